# Optimizing a Trainium2 kernel written in Bass

```python
import math
import jax, jax.numpy as jnp
from jax import lax
import numpy as np

D_MODEL = 1024
BATCH = 4
SEQ = 8192
DEPTH = 1

CHUNK = 64
N_META = 16
PAD = CHUNK - N_META
EPS = 1e-6
ROPE_BASE = 10000.0

RET_HEADS = 8
RET_DK = 128
RET_DV = 128
RET_QK = RET_HEADS * RET_DK
RET_V = RET_HEADS * RET_DV

SSD_D_INNER = D_MODEL
SSD_HEAD_DIM = 64
SSD_HEADS = SSD_D_INNER // SSD_HEAD_DIM
SSD_GROUPS = 4
SSD_HPG = SSD_HEADS // SSD_GROUPS
SSD_STATE = 128
SSD_CONV = 4
SSD_CONV_DIM = SSD_D_INNER + 2 * SSD_GROUPS * SSD_STATE

MIX_WIDTH = RET_V + SSD_D_INNER
IN_COLS = 2 * RET_QK + 2 * RET_V + SSD_D_INNER + SSD_CONV_DIM + SSD_HEADS
D_FF = 4 * D_MODEL

kernel_name = "hybrid_retention_ssd_meta_block"


def _rmsnorm(x, w):
    xf = x.astype(jnp.float32)
    y = xf * lax.rsqrt(jnp.mean(xf * xf, axis=-1, keepdims=True) + EPS)
    return y.astype(x.dtype) * w


def _split_cols(p):
    sizes = (RET_QK, RET_QK, RET_V, RET_V, SSD_D_INNER, SSD_CONV_DIM, SSD_HEADS)
    outs, start = [], 0
    for s in sizes:
        outs.append(p[..., start:start + s])
        start += s
    return outs


def _rope(t, pos):
    half = t.shape[-1] // 2
    freqs = ROPE_BASE ** (-jnp.arange(0, half, dtype=jnp.float32) / half)
    ang = pos[:, None] * freqs[None, :]
    cos = jnp.cos(ang)[None, :, None, :]
    sin = jnp.sin(ang)[None, :, None, :]
    t1, t2 = t[..., :half], t[..., half:]
    return jnp.concatenate([t1 * cos - t2 * sin, t1 * sin + t2 * cos], axis=-1)


def _retention(q, k, v, g, pos, ret_norm_w):
    b, l, _ = q.shape
    nc = l // CHUNK
    q = q.reshape(b, l, RET_HEADS, RET_DK)
    k = k.reshape(b, l, RET_HEADS, RET_DK)
    q = _rope(q, pos) * (RET_DK ** -0.5)
    k = _rope(k, pos)
    q = q.reshape(b, nc, CHUNK, RET_HEADS, RET_DK)
    k = k.reshape(b, nc, CHUNK, RET_HEADS, RET_DK)
    vv = v.reshape(b, nc, CHUNK, RET_HEADS, RET_DV)

    log_gamma = jnp.log1p(-jnp.exp2(-5.0 - jnp.arange(RET_HEADS, dtype=jnp.float32)))
    idx = jnp.arange(CHUNK, dtype=jnp.float32)
    dist = jnp.abs(idx[:, None] - idx[None, :])
    decay_intra = jnp.exp(log_gamma[:, None, None] * dist[None])

    scores = jnp.einsum('bcnhd,bcmhd->bhcnm', q, k) * decay_intra[None, :, None]
    o_intra = jnp.einsum('bhcnm,bcmhe->bcnhe', scores, vv)

    k_dec = k * jnp.exp((CHUNK - 1.0 - idx)[:, None] * log_gamma[None, :])[None, None, :, :, None]
    kv = jnp.einsum('bcmhd,bcmhe->cbhde', k_dec, vv).astype(jnp.float32)
    chunk_decay = jnp.exp(CHUNK * log_gamma)[None, :, None, None]

    def step(s, kv_c):
        return chunk_decay * s + kv_c, s

    _, s_prev = lax.scan(step, jnp.zeros((b, RET_HEADS, RET_DK, RET_DV), jnp.float32), kv)
    q_dec = q * jnp.exp((idx + 1.0)[:, None] * log_gamma[None, :])[None, None, :, :, None]
    o_inter = jnp.einsum('bcnhd,cbhde->bcnhe', q_dec, s_prev)

    o = (o_intra + o_inter).reshape(b, l, RET_HEADS, RET_DV).astype(jnp.float32)
    o = o * lax.rsqrt(jnp.mean(o * o, axis=-1, keepdims=True) + EPS)
    o = o.reshape(b, l, RET_V).astype(v.dtype) * ret_norm_w
    return o * jax.nn.silu(g)


def _ssd(z, xbc, dt_raw, valid, conv_w, conv_b, dt_bias, a_log, d_skip, norm_w):
    b, l, _ = z.shape
    nc = l // CHUNK
    conv = lax.conv_general_dilated(
        xbc, conv_w[:, None, :].astype(xbc.dtype), window_strides=(1,),
        padding=[(SSD_CONV - 1, 0)], dimension_numbers=('NWC', 'WIO', 'NWC'),
        feature_group_count=SSD_CONV_DIM)
    xbc = jax.nn.silu(conv + conv_b) * valid[None, :, None]
    xs = xbc[..., :SSD_D_INNER]
    bm = xbc[..., SSD_D_INNER:SSD_D_INNER + SSD_GROUPS * SSD_STATE]
    cm = xbc[..., SSD_D_INNER + SSD_GROUPS * SSD_STATE:]
    dt = jax.nn.softplus((dt_raw + dt_bias).astype(jnp.float32)) * valid[None, :, None]
    a = -jnp.exp(a_log.astype(jnp.float32))

    xs = xs.reshape(b, nc, CHUNK, SSD_GROUPS, SSD_HPG, SSD_HEAD_DIM)
    bm = bm.reshape(b, nc, CHUNK, SSD_GROUPS, SSD_STATE)
    cm = cm.reshape(b, nc, CHUNK, SSD_GROUPS, SSD_STATE)
    dt = dt.reshape(b, nc, CHUNK, SSD_GROUPS, SSD_HPG)
    a_cs = jnp.cumsum(dt * a.reshape(SSD_GROUPS, SSD_HPG), axis=2)
    xdt = xs * dt[..., None]

    seg = a_cs[:, :, :, None] - a_cs[:, :, None, :]
    causal = jnp.tril(jnp.ones((CHUNK, CHUNK), bool))[None, None, :, :, None, None]
    lmat = jnp.exp(jnp.where(causal, seg, -jnp.inf))
    cb = jnp.einsum('bclgn,bcsgn->bclsg', cm, bm)
    y_diag = jnp.einsum('bclsgr,bcsgrp->bclgrp', cb[..., None] * lmat, xdt)

    decay_states = jnp.exp(a_cs[:, :, -1:] - a_cs)
    states = jnp.einsum('bcsgn,bcsgrp->cbgrpn', bm, xdt * decay_states[..., None]).astype(jnp.float32)
    chunk_decay = jnp.transpose(jnp.exp(a_cs[:, :, -1]), (1, 0, 2, 3))[..., None, None]

    def step(h, inp):
        s_c, d_c = inp
        return d_c * h + s_c, h

    h0 = jnp.zeros((b, SSD_GROUPS, SSD_HPG, SSD_HEAD_DIM, SSD_STATE), jnp.float32)
    _, h_prev = lax.scan(step, h0, (states, chunk_decay))
    y_off = jnp.einsum('bclgn,cbgrpn->bclgrp', cm, h_prev) * jnp.exp(a_cs)[..., None]

    y = y_diag + y_off + xs * d_skip.reshape(SSD_GROUPS, SSD_HPG)[..., None]
    y = y.reshape(b, l, SSD_D_INNER) * jax.nn.silu(z)
    yf = y.astype(jnp.float32).reshape(b, l, SSD_GROUPS, SSD_D_INNER // SSD_GROUPS)
    yf = yf * lax.rsqrt(jnp.mean(yf * yf, axis=-1, keepdims=True) + EPS)
    return yf.reshape(b, l, SSD_D_INNER).astype(z.dtype) * norm_w


def setup_inputs(seed: int = 0) -> dict:
    key = jax.random.key(seed)
    ks = jax.random.split(key, 16)
    f32 = jnp.float32
    dt0 = jnp.exp(jax.random.uniform(ks[7], (DEPTH, SSD_HEADS), f32, math.log(1e-3), math.log(1e-1)))
    return {
        "x": jax.random.normal(ks[0], (BATCH, SEQ, D_MODEL), f32),
        "meta_tokens": jax.random.normal(ks[1], (N_META, D_MODEL), f32),
        "norm1_w": 1.0 + 0.02 * jax.random.normal(ks[2], (DEPTH, D_MODEL), f32),
        "w_in": jax.random.normal(ks[3], (DEPTH, D_MODEL, IN_COLS), f32) * D_MODEL ** -0.5,
        "ret_norm_w": 1.0 + 0.02 * jax.random.normal(ks[4], (DEPTH, RET_V), f32),
        "conv_w": jax.random.normal(ks[5], (DEPTH, SSD_CONV, SSD_CONV_DIM), f32) * SSD_CONV ** -0.5,
        "conv_b": 0.01 * jax.random.normal(ks[6], (DEPTH, SSD_CONV_DIM), f32),
        "dt_bias": dt0 + jnp.log(-jnp.expm1(-dt0)),
        "a_log": jnp.log(jax.random.uniform(ks[8], (DEPTH, SSD_HEADS), f32, 1.0, 16.0)),
        "d_skip": 1.0 + 0.02 * jax.random.normal(ks[9], (DEPTH, SSD_HEADS), f32),
        "ssd_norm_w": 1.0 + 0.02 * jax.random.normal(ks[10], (DEPTH, SSD_D_INNER), f32),
        "w_out": jax.random.normal(ks[11], (DEPTH, MIX_WIDTH, D_MODEL), f32) * MIX_WIDTH ** -0.5,
        "norm2_w": 1.0 + 0.02 * jax.random.normal(ks[12], (DEPTH, D_MODEL), f32),
        "w_ff1": jax.random.normal(ks[13], (DEPTH, D_MODEL, D_FF), f32) * D_MODEL ** -0.5,
        "w_ff2": jax.random.normal(ks[14], (DEPTH, D_FF, D_MODEL), f32) * D_FF ** -0.5,
        "final_norm_w": 1.0 + 0.02 * jax.random.normal(ks[15], (D_MODEL,), f32),
    }


def reference(x, meta_tokens, norm1_w, w_in, ret_norm_w, conv_w, conv_b, dt_bias, a_log,
              d_skip, ssd_norm_w, w_out, norm2_w, w_ff1, w_ff2, final_norm_w):
    b = x.shape[0]
    h = jnp.concatenate([
        jnp.zeros((b, PAD, D_MODEL), x.dtype),
        jnp.broadcast_to(meta_tokens.astype(x.dtype)[None], (b, N_META, D_MODEL)),
        x], axis=1)
    l = h.shape[1]
    idx = jnp.arange(l)
    valid = (idx >= PAD).astype(x.dtype)
    pos = (idx - PAD).astype(jnp.float32)

    for i in range(DEPTH):
        hn = _rmsnorm(h, norm1_w[i])
        proj = (hn @ w_in[i]) * valid[None, :, None]
        q, k, v, g, z, xbc, dt_raw = _split_cols(proj)
        y_ret = _retention(q, k, v, g, pos, ret_norm_w[i])
        y_ssd = _ssd(z, xbc, dt_raw, valid, conv_w[i], conv_b[i], dt_bias[i], a_log[i],
                     d_skip[i], ssd_norm_w[i])
        h = h + jnp.concatenate([y_ret, y_ssd], axis=-1) @ w_out[i]
        u = _rmsnorm(h, norm2_w[i]) @ w_ff1[i]
        h = h + jnp.square(jax.nn.relu(u)) @ w_ff2[i]

    return _rmsnorm(h, final_norm_w)[:, CHUNK:]
```

```python
import math
import numpy as np
import ml_dtypes
import concourse.bass as bass
import concourse.mybir as mybir
from concourse.bass_utils import run_bass_kernel_spmd

F32 = mybir.dt.float32
BF16 = mybir.dt.bfloat16
AF = mybir.ActivationFunctionType
ALU = mybir.AluOpType
AX = mybir.AxisListType

D = 1024
CH = 64
N_META = 16
PAD = CH - N_META
EPS = 1e-6
T = 128
NH = 8
DK = 128
IN_COLS = 7184
D_FF = 4096


class Res:
    __slots__ = ("name", "last_w", "readers", "excl")

    def __init__(self, name, excl=False):
        self.name = name
        self.excl = excl
        self.last_w = None
        self.readers = []


class Op:
    __slots__ = ("eng", "fn", "deps", "dma", "lane", "signal", "sem", "val", "idx", "cost", "marker", "t0", "desc")


class _Dummy:
    def then_inc(self, *a, **k):
        return self


class _Probe:
    def __init__(self):
        self.calls = []

    def __getattr__(self, name):
        def f(*a, **k):
            self.calls.append((name, a, k))
            return _Dummy()
        return f


def _nfree(ap):
    n = 1
    for d_ in ap.shape[1:]:
        n *= d_
    return n


def _probe_cost(eng, fn, dma):
    p = _Probe()
    fn(p)
    c = 0.0
    for name, a, k in p.calls:
        if name == "matmul":
            rhs = k.get("rhs", a[2] if len(a) > 2 else None)
            N = _nfree(rhs)
            c += (max(N, 64) / 2400.0 + 0.028) * (4.0 if rhs.dtype == F32 else 1.0)
        elif name == "transpose":
            c += 0.085
        elif name == "dma_start":
            out = k.get("out", a[0] if a else None)
            nbytes = _nfree(out) * out.shape[0] * (4 if out.dtype == F32 else 2)
            c += 2.0 + nbytes / 120e3
        else:
            out = k.get("out", a[0] if a else None)
            n = _nfree(out) if out is not None else 64
            if eng == "act":
                c += 0.22 + n / 1150.0
            elif eng == "dve":
                c += 0.14 + n / 940.0
            elif eng == "pool":
                c += 0.25 + n / 480.0
            else:
                c += 0.1
    return c


class Prog:
    ENGS = ("pe", "act", "dve", "pool", "sp")

    def __init__(self, nc):
        self.nc = nc
        self.ops = []
        self.lanes = {}
        self.group_lanes = set()
        self.out_dmas = []

    def add(self, eng, fn, reads=(), writes=(), dma=False, lane=None, group=False, is_out=False, n=128, cost=None):
        op = Op()
        op.eng, op.fn, op.dma, op.lane = eng, fn, dma, lane
        op.marker = False
        if cost is None:
            cost = _probe_cost(eng, fn, dma)
        op.cost = cost
        op.t0 = 0.0
        op.desc = ""
        op.signal = False
        op.sem = None
        op.val = 0
        op.idx = len(self.ops)
        deps = set()
        excl = [r for r in reads if r.excl]
        reads = [r for r in reads if not r.excl]
        writes = list(writes) + excl
        for r in reads:
            if r.last_w is not None:
                deps.add(r.last_w)
        for w in writes:
            if w.last_w is not None:
                deps.add(w.last_w)
            for rd in w.readers:
                deps.add(rd)
        deps.discard(op)
        op.deps = deps
        for r in reads:
            r.readers.append(op)
        for w in writes:
            w.last_w = op
            w.readers = []
        if dma:
            assert lane is not None
            self.lanes.setdefault(lane, []).append(op)
            if group:
                self.group_lanes.add(lane)
            if is_out:
                self.out_dmas.append(op)
        self.ops.append(op)
        return op

    def barrier(self):
        op = Op()
        op.eng, op.fn, op.dma, op.lane = None, None, False, None
        op.marker = True
        op.deps = set()
        op.signal = False
        op.sem = None
        op.val = 0
        op.cost = 0
        op.idx = len(self.ops)
        self.ops.append(op)

    def _list_schedule(self, seg):
        import heapq
        inseg = set(id(o) for o in seg)
        ndeps = {}
        users = {}
        for o in seg:
            c = 0
            for d in o.deps:
                if id(d) in inseg:
                    c += 1
                    users.setdefault(id(d), []).append(o)
            ndeps[id(o)] = c
        finish = {}
        ready_t = {}
        future = {e: [] for e in self.ENGS}
        avail = {e: [] for e in self.ENGS}
        free = {e: 0.0 for e in self.ENGS}
        for o in seg:
            if ndeps[id(o)] == 0:
                heapq.heappush(future[o.eng], (0.0, o.idx, o))
        order = []
        n_left = len(seg)
        while n_left:
            best = None
            for e in self.ENGS:
                if avail[e]:
                    t = free[e]
                elif future[e]:
                    t = max(free[e], future[e][0][0])
                else:
                    continue
                if best is None or t < best[0]:
                    best = (t, e)
            t, e = best
            while future[e] and future[e][0][0] <= t:
                rt, idx, o = heapq.heappop(future[e])
                heapq.heappush(avail[e], (idx, rt, o))
            idx, rt, o = heapq.heappop(avail[e])
            start = max(t, rt)
            if o.dma:
                free[e] = start + 0.06
                fin = start + o.cost
            else:
                free[e] = start + o.cost
                fin = free[e]
            finish[id(o)] = fin
            o.t0 = start
            order.append(o)
            n_left -= 1
            for u in users.get(id(o), ()):
                ndeps[id(u)] -= 1
                r = max(ready_t.get(id(u), 0.0), fin)
                ready_t[id(u)] = r
                if ndeps[id(u)] == 0:
                    heapq.heappush(future[u.eng], (r, u.idx, u))
        self.est = getattr(self, "est", 0.0) + max(list(finish.values()) + [0.0])
        return order

    def schedule(self, reorder=True):
        segs = [[]]
        for o in self.ops:
            if o.marker:
                segs.append([])
            else:
                segs[-1].append(o)
        result = []
        seen_lanes = {}
        lasts = None
        for seg in segs:
            order = self._list_schedule(seg) if reorder else seg
            if lasts is not None:
                for e in self.ENGS:
                    b = Op()
                    b.eng, b.fn, b.dma, b.lane = e, (lambda eng: eng.nop()), False, None
                    b.marker = False
                    b.deps = set(lasts)
                    b.signal = False
                    b.sem = None
                    b.val = 0
                    b.cost = 0
                    b.idx = -1
                    result.append(b)
            result += order
            lasts = []
            for e in self.ENGS:
                for o in reversed(order):
                    if o.eng == e and not o.dma:
                        lasts.append(o)
                        break
            for o in order:
                if o.dma:
                    seen_lanes[o.lane] = o
            lasts += list(seen_lanes.values())
        self.ops = result

    def emit(self):
        nc = self.nc
        for op in self.ops:
            for d in op.deps:
                if not (d.eng == "pe" and op.eng == "pe" and not d.dma and not op.dma):
                    d.signal = True
        for op in self.out_dmas:
            op.signal = True
        import contextlib
        with contextlib.ExitStack() as st:
            esem = {e: st.enter_context(nc.semaphore("sem_" + e)) for e in self.ENGS}
            lsem = {l: st.enter_context(nc.semaphore("lane_" + l)) for l in self.lanes}
            cnt = {e: 0 for e in self.ENGS}
            lcnt = {l: 0 for l in self.lanes}
            for op in self.ops:
                if op.dma:
                    lcnt[op.lane] += 1
                    op.sem = lsem[op.lane]
                    op.val = 16 * lcnt[op.lane]
                elif op.signal:
                    cnt[op.eng] += 1
                    op.sem = esem[op.eng]
                    op.val = cnt[op.eng]
            for l in self.group_lanes:
                for op in self.lanes[l]:
                    op.val = 16 * lcnt[l]
            by_eng = {e: [o for o in self.ops if o.eng == e] for e in self.ENGS}
            block = st.enter_context(nc.Block())

            def run(engine, e):
                waited = {}
                for op in by_eng[e]:
                    needs = {}
                    for d in op.deps:
                        if d.eng == "pe" and e == "pe" and not d.dma and not op.dma:
                            continue
                        k = d.sem
                        if d.val > needs.get(k, (0, None))[0]:
                            needs[k] = (d.val, d.sem)
                    for k, (v, s) in needs.items():
                        if waited.get(k, 0) < v:
                            engine.wait_ge(s, v)
                            waited[k] = v
                    ins = op.fn(engine)
                    if op.dma:
                        ins.then_inc(op.sem, 16)
                    elif op.signal:
                        ins.then_inc(op.sem, 1)
                if e == "sp":
                    last = {}
                    for op in self.out_dmas:
                        last[op.sem] = max(last.get(op.sem, (0, None))[0], op.val), op.sem
                    for k, (v, s) in last.items():
                        engine.wait_ge(s, v)

            @block.tensor
            def _(t):
                run(t, "pe")

            @block.scalar
            def _(a):
                run(a, "act")

            @block.vector
            def _(v):
                run(v, "dve")

            @block.gpsimd
            def _(g):
                run(g, "pool")

            @block.sync
            def _(s):
                run(s, "sp")


CST_COLS = {}


def _cst_layout():
    off = 0
    lay = {}
    for name, n in (("nw", 32), ("cw", 64), ("cb", 16), ("dtb", 16), ("alog", 16), ("dsk", 16),
                    ("U", 128), ("ones", 128), ("kd", 8), ("finw", 1024)):
        lay[name] = (off, off + n)
        off += n
    return lay, off


CST_LAY, CST_N = _cst_layout()
CSTB_N = 128 + 512 + 1024 + 1024 + 128


def host_constants():
    lg = np.log1p(-np.exp2(-5.0 - np.arange(NH, dtype=np.float64)))
    n = np.arange(T)[None, :]
    m = np.arange(T)[:, None]
    same = (n // CH) == (m // CH)
    cross = (n >= CH) & (m < CH)
    DT = np.zeros((T, NH, T), np.float64)
    for h in range(NH):
        d = np.where(same, np.exp(lg[h] * np.abs(n - m)), 0.0)
        d = np.where(cross, np.exp(lg[h] * (n - m)), d)
        DT[:, h, :] = d * DK ** -0.5
    Gq = np.zeros((T, NH, T), np.float64)
    for h in range(NH):
        Gq[:, h, :] = (np.exp(lg[h] * (np.arange(T) + 1.0)) * DK ** -0.5)[None, :]
    kd = np.exp(lg[None, :] * (T - 1.0 - np.arange(T))[:, None])
    sdec = np.exp(lg * T)
    U = (np.arange(T)[:, None] <= np.arange(T)[None, :]).astype(np.float64)
    maskneg = np.where(np.arange(T)[None, :] >= np.arange(T)[:, None], 0.0, -30000.0)
    ident = np.eye(T)
    cstb = np.concatenate([ident, np.tile(maskneg, (1, 4)), DT.reshape(T, -1), Gq.reshape(T, -1), np.ones((T, T))], axis=1)
    return dict(U=U.astype(np.float32), kd=kd.astype(np.float32), sdec=[float(v) for v in sdec],
                cstb=cstb.astype(ml_dtypes.bfloat16))


def build_program(NP, NF, sdec, phases=("R", "S", "C")):
    nc = bass.Bass("TRN2", target_bir_lowering=False)
    NTI = NP + NF
    xin = nc.dram_tensor("xin", [NTI * T, D], F32, kind="ExternalInput").ap()
    aux = nc.dram_tensor("aux", [NTI * T, 260], F32, kind="ExternalInput").ap()
    w_in = nc.dram_tensor("w_in", [D, IN_COLS], F32, kind="ExternalInput").ap()
    w_out = nc.dram_tensor("w_out", [2 * D, D], F32, kind="ExternalInput").ap()
    w_ff1 = nc.dram_tensor("w_ff1", [D, D_FF], F32, kind="ExternalInput").ap()
    w_ff2 = nc.dram_tensor("w_ff2", [D_FF, D], F32, kind="ExternalInput").ap()
    cst_d = nc.dram_tensor("cst", [T, CST_N], F32, kind="ExternalInput").ap()
    cstb_d = nc.dram_tensor("cstb", [T, CSTB_N], BF16, kind="ExternalInput").ap()
    out_d = nc.dram_tensor("out", [NF * T, D], F32, kind="ExternalOutput").ap()
    yrT_d = nc.dram_tensor("yrT_scr", [NF * T, 8 * T], BF16, kind="Internal").ap()
    hmid_d = nc.dram_tensor("hmid_scr", [NF * T, D], F32, kind="Internal").ap()

    P = Prog(nc)
    _n = [0]

    def sb(name, shape, dt):
        _n[0] += 1
        return nc.alloc_sbuf_tensor(f"{name}_{_n[0]}", list(shape), dt).ap()

    ARENA_BYTES = 168 * 1024
    arena = nc.alloc_sbuf_tensor("arena", [T, ARENA_BYTES // 2], BF16).ap()
    ws_off = [0]

    def ws_reset():
        ws_off[0] = 0

    def ws(name, shape, dt):
        esz = 4 if dt == F32 else 2
        n = 1
        for d_ in shape[1:]:
            n *= d_
        nb = (n * esz + 31) // 32 * 32
        o = ws_off[0]
        assert o + nb <= ARENA_BYTES, (name, o, nb)
        ws_off[0] = o + nb
        a = arena[:, o // 2:o // 2 + (n * esz) // 2]
        if dt == F32:
            a = a.bitcast(F32)
        if len(shape) == 3:
            a = a.rearrange("p (a b) -> p a b", a=shape[1])
        return a

    class Rot:
        def __init__(self, name, shape, dt, n, alloc=None):
            alloc = alloc or ws
            self.slots = [(Res(f"{name}{i}"), alloc(f"{name}{i}", shape, dt)) for i in range(n)]
            self.i = 0

        def next(self):
            s = self.slots[self.i % len(self.slots)]
            self.i += 1
            return s

    ps_all = nc.alloc_psum_tensor("ps_all", [T, 8, 512], F32).ap()
    ps_res = [Res(f"bank{i}", excl=True) for i in range(8)]
    ps_i = [0]

    def bank():
        b = ps_i[0] % 8
        ps_i[0] += 1
        return ps_res[b], ps_all[:, b, :]

    cst = sb("cst", [T, CST_N], F32)
    cstb = sb("cstb", [T, CSTB_N], BF16)
    R_cst, R_cstb = Res("cst"), Res("cstb")
    P.add("sp", lambda e: e.dma_start(out=cst[:], in_=cst_d[:, :]), writes=[R_cst], dma=True, lane="cst")
    P.add("sp", lambda e: e.dma_start(out=cstb[:], in_=cstb_d[:, :]), writes=[R_cstb], dma=True, lane="cstb")

    def C(name):
        a, b = CST_LAY[name]
        return cst[:, a:b]

    ident = cstb[:, 0:128]
    maskneg4 = cstb[:, 128:640]
    DTt = cstb[:, 640:1664].rearrange("p (h n) -> p h n", h=NH)
    Gq = cstb[:, 1664:2688].rearrange("p (h n) -> p h n", h=NH)
    onesb = cstb[:, 2688:2816]
    nw = C("nw")
    aneg = sb("aneg", [T, 16], F32)
    R_aneg = Res("aneg")
    P.add("act", lambda e: e.activation(out=aneg[:], in_=C("alog"), func=AF.Exp), reads=[R_cst], writes=[R_aneg])
    P.add("dve", lambda e: e.tensor_scalar(out=aneg[:], in0=aneg[:], scalar1=-1.0, scalar2=None, op0=ALU.mult),
          reads=[R_aneg], writes=[R_aneg])

    junk = sb("junk", [T, D], F32)
    R_junk = Res("junk")
    small = sb("small", [T, 64], F32)
    R_small = Res("small")

    def barrier():
        P.barrier()

    def load_weight(dst3, src, lane, row_chunks, col0, ncols):
        Rw = Res(lane)
        for kc in range(row_chunks):
            c = 0
            while c < ncols:
                n = min(2048, ncols - c)
                P.add("pool", (lambda e, kc=kc, c=c, n=n: e.dma_start(
                    out=dst3[:, kc, c:c + n], in_=src[kc * T:(kc + 1) * T, col0 + c:col0 + c + n])),
                    writes=[], dma=True, lane=lane, group=True)
                c += n
        last = P.lanes[lane][-1]
        Rw.last_w = last
        Rw.readers = []
        return Rw

    xrot = Rot("xt", [T, D], F32, 2, sb)
    arot = Rot("ax", [T, 260], F32, 2, sb)
    xnrot = Rot("xn", [T, D], BF16, 2, sb)
    hrot = Rot("hnT", [T, 8, T], BF16, 2, sb)
    strot = Rot("st", [T, 16], F32, 4, sb)
    dumprot = Rot("dump", [T, D], BF16, 2, sb)

    def rms_rstd(src_ap, R_src, nfeat):
        R_st, st = strot.next()
        R_dump, dump = dumprot.next()
        P.add("act", lambda e: e.activation(out=dump[:, 0:nfeat], in_=src_ap, func=AF.Square, accum_out=st[:, 0:1]),
              reads=[R_src], writes=[R_st, R_dump])
        P.add("act", lambda e: e.activation(out=st[:, 1:2], in_=st[:, 0:1], func=AF.Ln, scale=1.0 / nfeat, bias=EPS),
              reads=[R_st], writes=[R_st])
        P.add("act", lambda e: e.activation(out=st[:, 2:3], in_=st[:, 1:2], func=AF.Exp, scale=-0.5),
              reads=[R_st], writes=[R_st])
        return R_st, st

    def to_featmajor(src_bf, R_src, wcols, dst, R_dst, nblk=8):
        Rb, pb = bank()
        pv = pb.bitcast(BF16)

        def f(e):
            for kc in range(nblk):
                ins = e.transpose(out=pv[:, kc * T:(kc + 1) * T], in_=src_bf[:, kc * T:(kc + 1) * T], identity=ident)
            return ins
        P.add("pe", f, reads=[R_src, R_cstb], writes=[Rb])
        P.add("dve", lambda e: e.tensor_tensor(
            out=dst, in0=pv[:, 0:nblk * T].rearrange("p (k t) -> p k t", k=nblk),
            in1=wcols.unsqueeze(2).to_broadcast([T, nblk, T]), op=ALU.mult),
            reads=[Rb, R_cst], writes=[R_dst])

    def load_and_norm(ti, wcol0):
        Rx, xt = xrot.next()
        Ra, ax = arot.next()
        P.add("sp", lambda e: e.dma_start(out=xt[:], in_=xin[ti * T:(ti + 1) * T, :]), writes=[Rx], dma=True, lane=Rx.name)
        P.add("sp", lambda e: e.dma_start(out=ax[:], in_=aux[ti * T:(ti + 1) * T, :]), writes=[Ra], dma=True, lane=Ra.name)
        R_st, st = rms_rstd(xt[:], Rx, D)
        R_xn, xn = xnrot.next()
        P.add("act", lambda e: e.activation(out=xn[:], in_=xt[:], func=AF.Copy, scale=st[:, 2:3]),
              reads=[Rx, R_st], writes=[R_xn])
        Rh, hnT = hrot.next()
        to_featmajor(xn, R_xn, nw[:, wcol0:wcol0 + 8], hnT[:], Rh)
        return Rx, xt, Ra, ax, Rh, hnT

    def proj_tokmajor(Rh, hnT, w3, Rw, col0, ncols=512):
        Rb, pb = bank()

        def f(e):
            for kc in range(8):
                ins = e.matmul(pb[:, 0:ncols], lhsT=hnT[:, kc, :], rhs=w3[:, kc, col0:col0 + ncols],
                               start=(kc == 0), stop=(kc == 7))
            return ins
        P.add("pe", f, reads=[Rh, Rw], writes=[Rb])
        return Rb, pb

    if "R" in phases:
        ws_reset()
        wR = ws("wR", [T, 8, 4096], BF16)
        R_wR = load_weight(wR, w_in, "wR", 8, 0, 4096)
        qrrot = Rot("qr", [T, D], BF16, 2)
        krrot = Rot("kr", [T, D], BF16, 2)
        tA = ws("tA", [T, 4, 64], F32); tB = ws("tB", [T, 4, 64], F32)
        R_tA, R_tB = Res("tA"), Res("tB")
        vrot = Rot("v", [T, D], BF16, 2)
        sgrot = Rot("sg", [T, D], BF16, 2)
        qTrot = Rot("qT", [T, NH, T], BF16, 2)
        qdTrot = Rot("qdT", [T, NH, T], BF16, 2)
        kTrot = Rot("kT", [T, NH, T], BF16, 2)
        kdrot = Rot("kdec", [T, D], BF16, 2)
        PTrot = Rot("PT", [T, NH, T], BF16, 2)
        S = ws("S", [T, NH, T], F32); R_S = Res("S")
        Sbf = ws("Sbf", [T, NH, T], BF16); R_Sbf = Res("Sbf")
        ss8rot = Rot("ss8", [T, 16], F32, 2)
        ytrot = Rot("ytmp", [T, D], F32, 2)
        yrrot_ = Rot("yr", [T, D], BF16, 2)
        yTrot = Rot("yT", [T, NH, T], BF16, 2)
        P.add("dve", lambda e: e.memset(S[:], 0.0), writes=[R_S])
        P.add("dve", lambda e: e.memset(Sbf[:], 0.0), writes=[R_Sbf])

        tC = ws("tC", [T, 4, 64], F32); tD = ws("tD", [T, 4, 64], F32)
        R_tC, R_tD = Res("tC"), Res("tD")
        qrawrot = Rot("qraw", [T, 512], F32, 2)

        def rope(Rb, pb, dst, R_dst, blk, Ra, ax, eng="dve"):
            if eng == "dve":
                tA_, tB_, R_tA_, R_tB_ = tA, tB, R_tA, R_tB
            else:
                tA_, tB_, R_tA_, R_tB_ = tC, tD, R_tC, R_tD
            pv = pb.rearrange("p (h two d) -> p h two d", h=4, two=2)
            a, b = pv[:, :, 0, :], pv[:, :, 1, :]
            cos = ax[:, 0:64].unsqueeze(1).to_broadcast([T, 4, 64])
            sin = ax[:, 64:128].unsqueeze(1).to_broadcast([T, 4, 64])
            dv = dst[:, blk * 512:(blk + 1) * 512].rearrange("p (h two d) -> p h two d", h=4, two=2)
            P.add(eng, lambda e: e.tensor_tensor(out=tA_[:], in0=a, in1=cos, op=ALU.mult), reads=[Rb, Ra], writes=[R_tA_])
            P.add(eng, lambda e: e.tensor_tensor(out=tB_[:], in0=b, in1=sin, op=ALU.mult), reads=[Rb, Ra], writes=[R_tB_])
            P.add(eng, lambda e: e.tensor_tensor(out=dv[:, :, 0, :], in0=tA_[:], in1=tB_[:], op=ALU.subtract),
                  reads=[R_tA_, R_tB_], writes=[R_dst])
            P.add(eng, lambda e: e.tensor_tensor(out=tA_[:], in0=a, in1=sin, op=ALU.mult), reads=[Rb, Ra], writes=[R_tA_])
            P.add(eng, lambda e: e.tensor_tensor(out=tB_[:], in0=b, in1=cos, op=ALU.mult), reads=[Rb, Ra], writes=[R_tB_])
            P.add(eng, lambda e: e.tensor_tensor(out=dv[:, :, 1, :], in0=tA_[:], in1=tB_[:], op=ALU.add),
                  reads=[R_tA_, R_tB_], writes=[R_dst])

        def tile_R(ti, full, out_i):
            Rx, xt, Ra, ax, Rh, hnT = load_and_norm(ti, 0)
            R_qr, qr = qrrot.next()
            R_kr, kr = krrot.next()
            R_qT, qT = qTrot.next()
            R_qdT, qdT = qdTrot.next()
            R_kT, kT = kTrot.next()
            R_kdec, kdec = kdrot.next()
            R_PT, PT = PTrot.next()
            R_ss8, ss8 = ss8rot.next()
            R_ytmp, ytmp = ytrot.next()
            R_yr, yr = yrrot_.next()
            Rv, v = vrot.next()
            Rsg, sg = sgrot.next()
            if full:
                for blk in range(2):
                    Rb, pb = proj_tokmajor(Rh, hnT, wR, R_wR, blk * 512)
                    Rq_, qraw = qrawrot.next()
                    P.add("act", lambda e, pb=pb, qraw=qraw: e.activation(out=qraw[:], in_=pb, func=AF.Copy),
                          reads=[Rb], writes=[Rq_])
                    rope(Rq_, qraw, qr, R_qr, blk, Ra, ax, eng="pool")
            for blk in range(2):
                Rb, pb = proj_tokmajor(Rh, hnT, wR, R_wR, 1024 + blk * 512)
                rope(Rb, pb, kr, R_kr, blk, Ra, ax)
            for blk in range(2):
                Rb, pb = proj_tokmajor(Rh, hnT, wR, R_wR, 2048 + blk * 512)
                P.add("act", lambda e, pb=pb, blk=blk: e.activation(out=v[:, blk * 512:(blk + 1) * 512], in_=pb, func=AF.Copy),
                      reads=[Rb], writes=[Rv])
            if full:
                for blk in range(2):
                    Rb, pb = proj_tokmajor(Rh, hnT, wR, R_wR, 3072 + blk * 512)
                    P.add("act", lambda e, pb=pb, blk=blk: e.activation(out=sg[:, blk * 512:(blk + 1) * 512], in_=pb, func=AF.Silu),
                          reads=[Rb], writes=[Rsg])
            P.add("dve", lambda e: e.tensor_tensor(
                out=kdec[:].rearrange("p (h d) -> p h d", h=NH), in0=kr[:].rearrange("p (h d) -> p h d", h=NH),
                in1=C("kd").unsqueeze(2).to_broadcast([T, NH, DK]), op=ALU.mult),
                reads=[R_kr, R_cst], writes=[R_kdec])
            if full:
                Rb, pb = bank()
                pv = pb.bitcast(BF16)

                def f(e, pv=pv):
                    for h in range(NH):
                        ins = e.transpose(out=pv[:, h * T:(h + 1) * T], in_=qr[:, h * DK:(h + 1) * DK], identity=ident)
                    return ins
                P.add("pe", f, reads=[R_qr, R_cstb], writes=[Rb])
                P.add("act", lambda e, pv=pv: e.activation(out=qT[:].rearrange("p h n -> p (h n)"), in_=pv, func=AF.Copy),
                      reads=[Rb], writes=[R_qT])
                P.add("dve", lambda e, pv=pv: e.tensor_tensor(out=qdT[:], in0=pv.rearrange("p (h n) -> p h n", h=NH),
                                                             in1=Gq, op=ALU.mult),
                      reads=[Rb, R_cstb], writes=[R_qdT])
                Rb, pb = bank()
                pv = pb.bitcast(BF16)

                def f(e, pv=pv):
                    for h in range(NH):
                        ins = e.transpose(out=pv[:, h * T:(h + 1) * T], in_=kr[:, h * DK:(h + 1) * DK], identity=ident)
                    return ins
                P.add("pe", f, reads=[R_kr, R_cstb], writes=[Rb])
                P.add("act", lambda e, pv=pv: e.activation(out=kT[:].rearrange("p h n -> p (h n)"), in_=pv, func=AF.Copy),
                      reads=[Rb], writes=[R_kT])
                for half in range(2):
                    Rb, pb = bank()

                    def f(e, pb=pb, half=half):
                        for j in range(4):
                            h = half * 4 + j
                            ins = e.matmul(pb[:, j * T:(j + 1) * T], lhsT=kT[:, h, :], rhs=qT[:, h, :], start=True, stop=True)
                        return ins
                    P.add("pe", f, reads=[R_kT, R_qT], writes=[Rb])
                    P.add("dve", lambda e, pb=pb, half=half: e.tensor_tensor(
                        out=PT[:, half * 4:half * 4 + 4, :], in0=pb.rearrange("p (h n) -> p h n", h=4),
                        in1=DTt[:, half * 4:half * 4 + 4, :], op=ALU.mult),
                        reads=[Rb, R_cstb], writes=[R_PT])
                obanks = []
                for half in range(2):
                    Rb, pb = bank()

                    def f(e, pb=pb, half=half):
                        for j in range(4):
                            h = half * 4 + j
                            e.matmul(pb[:, j * T:(j + 1) * T], lhsT=PT[:, h, :], rhs=v[:, h * DK:(h + 1) * DK],
                                     start=True, stop=False, skip_group_check=True)
                            ins = e.matmul(pb[:, j * T:(j + 1) * T], lhsT=qdT[:, h, :], rhs=Sbf[:, h, :],
                                           start=False, stop=True, skip_group_check=True)
                        return ins
                    P.add("pe", f, reads=[R_PT, Rv, R_qdT, R_Sbf], writes=[Rb])
                    obanks.append((Rb, pb))
            for half in range(2):
                Rb, pb = bank()

                def f(e, pb=pb, half=half):
                    for j in range(4):
                        h = half * 4 + j
                        ins = e.matmul(pb[:, j * T:(j + 1) * T], lhsT=kdec[:, h * DK:(h + 1) * DK], rhs=v[:, h * DK:(h + 1) * DK],
                                       start=True, stop=True)
                    return ins
                P.add("pe", f, reads=[R_kdec, Rv], writes=[Rb])
                for j in range(4):
                    h = half * 4 + j
                    P.add("dve", lambda e, pb=pb, j=j, h=h: e.scalar_tensor_tensor(
                        out=S[:, h, :], in0=S[:, h, :], scalar=sdec[h], in1=pb[:, j * T:(j + 1) * T],
                        op0=ALU.mult, op1=ALU.add), reads=[Rb, R_S], writes=[R_S])
            P.add("act", lambda e: e.activation(out=Sbf[:].rearrange("p h n -> p (h n)"),
                                                in_=S[:].rearrange("p h n -> p (h n)"), func=AF.Copy),
                  reads=[R_S], writes=[R_Sbf])
            if full:
                for half, (Rb, pb) in enumerate(obanks):
                    P.add("act", lambda e, pb=pb, half=half: e.activation(out=junk[:, half * 512:(half + 1) * 512], in_=pb, func=AF.Square),
                          reads=[Rb], writes=[R_junk])
                P.add("dve", lambda e: e.tensor_reduce(out=ss8[:, 0:8], in_=junk[:].rearrange("p (h d) -> p h d", h=NH),
                                                       axis=AX.X, op=ALU.add), reads=[R_junk], writes=[R_ss8])
                P.add("act", lambda e: e.activation(out=ss8[:, 8:16], in_=ss8[:, 0:8], func=AF.Ln, scale=1.0 / DK, bias=EPS),
                      reads=[R_ss8], writes=[R_ss8])
                P.add("act", lambda e: e.activation(out=ss8[:, 0:8], in_=ss8[:, 8:16], func=AF.Exp, scale=-0.5),
                      reads=[R_ss8], writes=[R_ss8])
                for half, (Rb, pb) in enumerate(obanks):
                    P.add("dve", lambda e, pb=pb, half=half: e.tensor_tensor(
                        out=ytmp[:, half * 512:(half + 1) * 512].rearrange("p (h d) -> p h d", h=4),
                        in0=pb.rearrange("p (h d) -> p h d", h=4),
                        in1=ss8[:, half * 4:half * 4 + 4].unsqueeze(2).to_broadcast([T, 4, DK]), op=ALU.mult),
                        reads=[Rb, R_ss8], writes=[R_ytmp])
                P.add("dve", lambda e: e.tensor_tensor(out=yr[:], in0=ytmp[:], in1=sg[:], op=ALU.mult),
                      reads=[R_ytmp, Rsg], writes=[R_yr])
                RyT, yT = yTrot.next()
                to_featmajor(yr, R_yr, nw[:, 8:16], yT[:], RyT)
                P.add("sp", lambda e: e.dma_start(out=yrT_d[out_i * T:(out_i + 1) * T, :], in_=yT[:].rearrange("p h n -> p (h n)")),
                      reads=[RyT], dma=True, lane="o_" + RyT.name)

        for i in range(NP):
            tile_R(i, False, None)
        for i in range(NF):
            tile_R(NP + i, True, i)
        barrier()

    if "S" in phases:
        ws_reset()
        wS = ws("wS", [T, 8, 3088], BF16)
        wO = ws("wO", [T, 16, 1024], BF16)
        R_wS = load_weight(wS, w_in, "wS", 8, 4096, 3088)
        R_wO = load_weight(wO, w_out, "wO", 16, 0, 1024)
        szrot = Rot("sz", [T, D], BF16, 2)
        xcrot = Rot("xcb", [T, 4, 131], BF16, 2)
        halo = ws("halo", [T, 16, 3], BF16); R_halo = Res("halo")
        diag = ws("diag", [T, 80, T], BF16); R_diag = Res("diag")
        xcsrot = Rot("xcs", [T, 16, T], BF16, 2)
        dtsrot = Rot("dts", [T, 128], F32, 2)
        xs_tm = ws("xs_tm", [T, D], BF16); R_xs = Res("xs_tm")
        xdtrot = Rot("xdt", [T, D], BF16, 2)
        xdtdrot = Rot("xdtd", [T, D], BF16, 2)
        Btmrot = Rot("Btm", [T, 512], BF16, 2)
        Rrot = Rot("Rr", [T, 4, T], F32, 2)
        LT = ws("LT", [T, 16, T], BF16); R_LT = Res("LT")
        MT = ws("MT", [T, 16, T], BF16); R_MT = Res("MT")
        H = ws("H", [T, D], F32); R_H = Res("H")
        Hbf = ws("Hbf", [T, D], BF16); R_Hbf = Res("Hbf")
        y1 = ws("y1", [T, D], F32); R_y1 = Res("y1")
        y2 = junk; R_y2 = R_junk
        ynb = ws("ynb", [T, D], BF16); R_ynb = Res("ynb")
        ysTrot = Rot("ysT", [T, 8, T], BF16, 2)
        yrrot = Rot("yrT", [T, 8, T], BF16, 2)
        hmrot = Rot("hm", [T, D], F32, 1)
        P.add("dve", lambda e: e.memset(H[:], 0.0), writes=[R_H])
        P.add("dve", lambda e: e.memset(Hbf[:], 0.0), writes=[R_Hbf])
        P.add("dve", lambda e: e.memset(halo[:], 0.0), writes=[R_halo])
        cw = C("cw").rearrange("p (b j) -> p b j", j=4)
        cbias = C("cb")
        for blk in range(16):
            for k in range(5):
                sc = cw[:, blk, k:k + 1] if k < 4 else cbias[:, blk:blk + 1]
                P.add("dve", lambda e, blk=blk, k=k, sc=sc: e.tensor_scalar(
                    out=diag[:, blk * 5 + k, :], in0=ident, scalar1=sc, scalar2=None, op0=ALU.mult),
                    reads=[R_cst, R_cstb], writes=[R_diag])

        def tile_S(ti, full, out_i, mask, allgrp=False):
            Rx, xt, Ra, ax, Rh, hnT = load_and_norm(ti, 0)
            R_xcs, xcs = xcsrot.next()
            R_dts, dts = dtsrot.next()
            R_xdt, xdt = xdtrot.next()
            R_xdtd, xdtd = xdtdrot.next()
            R_Btm, Btm = Btmrot.next()
            R_ysT, ysT = ysTrot.next()
            Rsz, sz = szrot.next()
            Rb, pb = proj_tokmajor(Rh, hnT, wS, R_wS, 3072, 16)
            P.add("dve", lambda e, pb=pb: e.tensor_tensor(out=dts[:, 0:16], in0=pb[:, 0:16], in1=C("dtb"), op=ALU.add),
                  reads=[Rb, R_cst], writes=[R_dts])
            P.add("act", lambda e: e.activation(out=dts[:, 0:16], in_=dts[:, 0:16], func=AF.Exp), reads=[R_dts], writes=[R_dts])
            P.add("act", lambda e: e.activation(out=dts[:, 0:16], in_=dts[:, 0:16], func=AF.Ln, bias=1.0), reads=[R_dts], writes=[R_dts])
            P.add("dve", lambda e: e.tensor_scalar(out=dts[:, 16:32], in0=dts[:, 0:16], scalar1=ax[:, 128:129], scalar2=None, op0=ALU.mult),
                  reads=[R_dts, Ra], writes=[R_dts])
            P.add("dve", lambda e: e.tensor_tensor(out=dts[:, 32:48], in0=dts[:, 16:32], in1=aneg[:], op=ALU.mult),
                  reads=[R_dts, R_aneg], writes=[R_dts])
            Rb, pb = bank()

            def f(e, pb=pb):
                e.matmul(pb[:, 0:16], lhsT=C("U"), rhs=dts[:, 32:48], start=True, stop=True)
                return e.matmul(pb[:, 16:32], lhsT=C("ones"), rhs=dts[:, 32:48], start=True, stop=True)
            P.add("pe", f, reads=[R_dts, R_cst], writes=[Rb])
            P.add("act", lambda e, pb=pb: e.activation(out=dts[:, 48:64], in_=pb[:, 0:16], func=AF.Copy), reads=[Rb], writes=[R_dts])
            P.add("act", lambda e, pb=pb: e.activation(out=dts[:, 64:80], in_=pb[:, 0:16], func=AF.Copy, scale=-1.0), reads=[Rb], writes=[R_dts])
            P.add("dve", lambda e, pb=pb: e.tensor_tensor(out=dts[:, 80:96], in0=pb[:, 16:32], in1=dts[:, 48:64], op=ALU.subtract),
                  reads=[Rb, R_dts], writes=[R_dts])
            P.add("act", lambda e: e.activation(out=dts[:, 80:96], in_=dts[:, 80:96], func=AF.Exp), reads=[R_dts], writes=[R_dts])
            P.add("act", lambda e, pb=pb: e.activation(out=dts[:, 96:112], in_=pb[:, 16:32], func=AF.Exp), reads=[Rb], writes=[R_dts])
            if full:
                P.add("act", lambda e: e.activation(out=dts[:, 112:128], in_=dts[:, 48:64], func=AF.Exp), reads=[R_dts], writes=[R_dts])
            if full:
                for blk in range(2):
                    Rb, pb = proj_tokmajor(Rh, hnT, wS, R_wS, blk * 512)
                    P.add("act", lambda e, pb=pb, blk=blk: e.activation(out=sz[:, blk * 512:(blk + 1) * 512], in_=pb, func=AF.Silu),
                          reads=[Rb], writes=[Rsz])
            ngrp = 4 if (full or allgrp) else 3
            for g4 in range(ngrp):
                Rb, pb = bank()

                def f(e, pb=pb, g4=g4):
                    for j in range(4):
                        c0 = 1024 + (g4 * 4 + j) * T
                        for kc in range(8):
                            ins = e.matmul(pb[:, j * T:(j + 1) * T], lhsT=wS[:, kc, c0:c0 + T], rhs=hnT[:, kc, :],
                                           start=(kc == 0), stop=(kc == 7), skip_group_check=True)
                    return ins
                P.add("pe", f, reads=[Rh, R_wS], writes=[Rb])
                Rxc, xc = xcrot.next()
                P.add("act", lambda e, pb=pb, xc=xc: e.activation(out=xc[:, :, 3:131], in_=pb.rearrange("p (j t) -> p j t", j=4), func=AF.Copy),
                      reads=[Rb], writes=[Rxc])
                P.add("pool", lambda e, xc=xc, g4=g4: e.tensor_copy(out=xc[:, :, 0:3], in_=halo[:, g4 * 4:g4 * 4 + 4, :]),
                      reads=[R_halo], writes=[Rxc])
                P.add("pool", lambda e, xc=xc, g4=g4: e.tensor_copy(out=halo[:, g4 * 4:g4 * 4 + 4, :], in_=xc[:, :, 128:131]),
                      reads=[Rxc], writes=[R_halo])
                Rc, pc = bank()

                def f(e, pc=pc, xc=xc, g4=g4):
                    for j in range(4):
                        blk = g4 * 4 + j
                        for k in range(4):
                            e.matmul(pc[:, j * T:(j + 1) * T], lhsT=diag[:, blk * 5 + k, :], rhs=xc[:, j, k:k + T],
                                     start=(k == 0), stop=False, skip_group_check=True)
                        ins = e.matmul(pc[:, j * T:(j + 1) * T], lhsT=diag[:, blk * 5 + 4, :], rhs=onesb,
                                       start=False, stop=True, skip_group_check=True)
                    return ins
                P.add("pe", f, reads=[Rxc, R_diag, R_cstb], writes=[Rc])
                if mask:
                    P.add("act", lambda e, pc=pc: e.activation(out=y2[:, 0:512], in_=pc, func=AF.Silu), reads=[Rc], writes=[R_y2])
                    P.add("dve", lambda e, g4=g4, xcs=xcs: e.tensor_tensor(
                        out=xcs[:, g4 * 4:g4 * 4 + 4, :], in0=y2[:, 0:512].rearrange("p (j t) -> p j t", j=4),
                        in1=ax[:, 132:260].unsqueeze(1).to_broadcast([T, 4, T]),
                        op=ALU.mult), reads=[R_y2, Ra], writes=[R_xcs])
                else:
                    P.add("act", lambda e, pc=pc, g4=g4, xcs=xcs: e.activation(
                        out=xcs[:, g4 * 4:g4 * 4 + 4, :].rearrange("p j t -> p (j t)"), in_=pc, func=AF.Silu),
                        reads=[Rc], writes=[R_xcs])
            Rb, pb = bank()
            pv = pb.bitcast(BF16)

            def f(e, pv=pv):
                for j in range(8):
                    ins = e.transpose(out=pv[:, j * T:(j + 1) * T], in_=xcs[:, j, :], identity=ident)
                return ins
            P.add("pe", f, reads=[R_xcs, R_cstb], writes=[Rb])
            if full:
                P.add("act", lambda e, pv=pv: e.activation(out=xs_tm[:], in_=pv, func=AF.Copy), reads=[Rb], writes=[R_xs])
            P.add("dve", lambda e, pv=pv: e.tensor_tensor(
                out=xdt[:].rearrange("p (h d) -> p h d", h=16), in0=pv.rearrange("p (h d) -> p h d", h=16),
                in1=dts[:, 16:32].unsqueeze(2).to_broadcast([T, 16, 64]), op=ALU.mult),
                reads=[Rb, R_dts], writes=[R_xdt])
            P.add("dve", lambda e: e.tensor_tensor(
                out=xdtd[:].rearrange("p (h d) -> p h d", h=16), in0=xdt[:].rearrange("p (h d) -> p h d", h=16),
                in1=dts[:, 80:96].unsqueeze(2).to_broadcast([T, 16, 64]), op=ALU.mult),
                reads=[R_xdt, R_dts], writes=[R_xdtd])
            Rb, pb = bank()
            pv = pb.bitcast(BF16)

            def f(e, pv=pv):
                for j in range(4):
                    ins = e.transpose(out=pv[:, j * T:(j + 1) * T], in_=xcs[:, 8 + j, :], identity=ident)
                return ins
            P.add("pe", f, reads=[R_xcs, R_cstb], writes=[Rb])
            P.add("act", lambda e, pv=pv: e.activation(out=Btm[:], in_=pv[:, 0:512], func=AF.Copy), reads=[Rb], writes=[R_Btm])
            if full:
                for g4 in range(4):
                    Rr, Rt = Rrot.next()
                    P.add("pool", lambda e, Rt=Rt, g4=g4: e.tensor_tensor(
                        out=Rt[:], in0=C("U").unsqueeze(1).to_broadcast([T, 4, T]),
                        in1=dts[:, 32 + g4 * 4:32 + g4 * 4 + 4].unsqueeze(2).to_broadcast([T, 4, T]), op=ALU.mult),
                        reads=[R_cst, R_dts], writes=[Rr])
                    Rb, pb = bank()

                    def f(e, pb=pb, Rt=Rt):
                        e.matmul(pb, lhsT=C("ones"), rhs=Rt[:].rearrange("p h l -> p (h l)"), start=True, stop=False)
                        return e.matmul(pb, lhsT=ident, rhs=maskneg4, start=False, stop=True)
                    P.add("pe", f, reads=[Rr, R_cst, R_cstb], writes=[Rb])
                    for j in range(4):
                        hh = g4 * 4 + j
                        P.add("act", lambda e, pb=pb, j=j, hh=hh: e.activation(
                            out=LT[:, hh, :], in_=pb[:, j * T:(j + 1) * T], func=AF.Exp, bias=dts[:, 64 + hh:65 + hh]),
                            reads=[Rb, R_dts], writes=[R_LT])
                Rb, pb = bank()

                def f(e, pb=pb):
                    for g in range(4):
                        ins = e.matmul(pb[:, g * T:(g + 1) * T], lhsT=xcs[:, 8 + g, :], rhs=xcs[:, 12 + g, :], start=True, stop=True)
                    return ins
                P.add("pe", f, reads=[R_xcs], writes=[Rb])
                for g in range(4):
                    P.add("dve", lambda e, pb=pb, g=g: e.tensor_tensor(
                        out=MT[:, g * 4:g * 4 + 4, :], in0=LT[:, g * 4:g * 4 + 4, :],
                        in1=pb[:, g * T:(g + 1) * T].unsqueeze(1).to_broadcast([T, 4, T]), op=ALU.mult),
                        reads=[Rb, R_LT], writes=[R_MT])
                for half in range(2):
                    RbY, pbY = bank()

                    def f(e, pb=pbY, half=half):
                        for j in range(8):
                            h = half * 8 + j
                            ins = e.matmul(pb[:, j * 64:(j + 1) * 64], lhsT=MT[:, h, :], rhs=xdt[:, h * 64:(h + 1) * 64],
                                           start=True, stop=True)
                        return ins
                    P.add("pe", f, reads=[R_MT, R_xdt], writes=[RbY])
                    RbO, pbO = bank()

                    def f(e, pb=pbO, half=half):
                        for j in range(2):
                            g = half * 2 + j
                            ins = e.matmul(pb[:, j * 256:(j + 1) * 256], lhsT=xcs[:, 12 + g, :], rhs=Hbf[:, g * 256:(g + 1) * 256],
                                           start=True, stop=True)
                        return ins
                    P.add("pe", f, reads=[R_xcs, R_Hbf], writes=[RbO])
                    hs = slice(half * 512, (half + 1) * 512)
                    P.add("dve", lambda e, pbO=pbO, half=half, hs=hs: e.tensor_tensor(
                        out=y1[:, hs].rearrange("p (h d) -> p h d", h=8), in0=pbO.rearrange("p (h d) -> p h d", h=8),
                        in1=dts[:, 112 + half * 8:120 + half * 8].unsqueeze(2).to_broadcast([T, 8, 64]), op=ALU.mult),
                        reads=[RbO, R_dts], writes=[R_y1])
                    P.add("dve", lambda e, pbY=pbY, hs=hs: e.tensor_tensor(out=y1[:, hs], in0=pbY, in1=y1[:, hs], op=ALU.add),
                          reads=[RbY, R_y1], writes=[R_y1])
                    P.add("pool", lambda e, half=half, hs=hs: e.tensor_tensor(
                        out=y2[:, hs].rearrange("p (h d) -> p h d", h=8), in0=xs_tm[:, hs].rearrange("p (h d) -> p h d", h=8),
                        in1=C("dsk")[:, half * 8:half * 8 + 8].unsqueeze(2).to_broadcast([T, 8, 64]), op=ALU.mult),
                        reads=[R_xs, R_cst], writes=[R_y2])
                    P.add("pool", lambda e, hs=hs: e.tensor_tensor(out=y1[:, hs], in0=y1[:, hs], in1=y2[:, hs], op=ALU.add),
                          reads=[R_y1, R_y2], writes=[R_y1])
                    P.add("pool", lambda e, hs=hs: e.tensor_tensor(out=y1[:, hs], in0=y1[:, hs], in1=sz[:, hs], op=ALU.mult),
                          reads=[R_y1, Rsz], writes=[R_y1])
            for half in range(2):
                Rb, pb = bank()

                def f(e, pb=pb, half=half):
                    for j in range(2):
                        g = half * 2 + j
                        ins = e.matmul(pb[:, j * 256:(j + 1) * 256], lhsT=Btm[:, g * T:(g + 1) * T], rhs=xdtd[:, g * 256:(g + 1) * 256],
                                       start=True, stop=True)
                    return ins
                P.add("pe", f, reads=[R_Btm, R_xdtd], writes=[Rb])
                hs = slice(half * 512, (half + 1) * 512)
                P.add("dve", lambda e, half=half, hs=hs: e.tensor_tensor(
                    out=H[:, hs].rearrange("p (h d) -> p h d", h=8), in0=H[:, hs].rearrange("p (h d) -> p h d", h=8),
                    in1=dts[:, 96 + half * 8:104 + half * 8].unsqueeze(2).to_broadcast([T, 8, 64]), op=ALU.mult),
                    reads=[R_H, R_dts], writes=[R_H])
                P.add("dve", lambda e, pb=pb, hs=hs: e.tensor_tensor(out=H[:, hs], in0=pb, in1=H[:, hs], op=ALU.add),
                      reads=[Rb, R_H], writes=[R_H])
            P.add("act", lambda e: e.activation(out=Hbf[:], in_=H[:], func=AF.Copy), reads=[R_H], writes=[R_Hbf])
            if full:
                P.add("act", lambda e: e.activation(out=junk[:], in_=y1[:], func=AF.Square), reads=[R_y1], writes=[R_junk])
                P.add("dve", lambda e: e.tensor_reduce(out=small[:, 0:4], in_=junk[:].rearrange("p (g d) -> p g d", g=4),
                                                       axis=AX.X, op=ALU.add), reads=[R_junk], writes=[R_small])
                P.add("act", lambda e: e.activation(out=small[:, 4:8], in_=small[:, 0:4], func=AF.Ln, scale=1.0 / 256, bias=EPS),
                      reads=[R_small], writes=[R_small])
                P.add("act", lambda e: e.activation(out=small[:, 8:12], in_=small[:, 4:8], func=AF.Exp, scale=-0.5),
                      reads=[R_small], writes=[R_small])
                P.add("dve", lambda e: e.tensor_tensor(
                    out=ynb[:].rearrange("p (g d) -> p g d", g=4), in0=y1[:].rearrange("p (g d) -> p g d", g=4),
                    in1=small[:, 8:12].unsqueeze(2).to_broadcast([T, 4, 256]), op=ALU.mult),
                    reads=[R_y1, R_small], writes=[R_ynb])
                to_featmajor(ynb, R_ynb, nw[:, 16:24], ysT[:], R_ysT)
                Ryr, yrT = yrrot.next()
                P.add("sp", lambda e: e.dma_start(out=yrT[:].rearrange("p h n -> p (h n)"), in_=yrT_d[out_i * T:(out_i + 1) * T, :]),
                      writes=[Ryr], dma=True, lane=Ryr.name)
                Rhm, hm = hmrot.next()
                for nb in range(2):
                    Rb, pb = bank()

                    def f(e, pb=pb, nb=nb, yrT=yrT):
                        for kc in range(16):
                            l = yrT[:, kc, :] if kc < 8 else ysT[:, kc - 8, :]
                            ins = e.matmul(pb, lhsT=l, rhs=wO[:, kc, nb * 512:(nb + 1) * 512], start=(kc == 0), stop=(kc == 15))
                        return ins
                    P.add("pe", f, reads=[Ryr, R_ysT, R_wO], writes=[Rb])
                    P.add("dve", lambda e, pb=pb, nb=nb, hm=hm: e.tensor_tensor(
                        out=hm[:, nb * 512:(nb + 1) * 512], in0=pb, in1=xt[:, nb * 512:(nb + 1) * 512], op=ALU.add),
                        reads=[Rb, Rx], writes=[Rhm])
                P.add("sp", lambda e, hm=hm: e.dma_start(out=hmid_d[out_i * T:(out_i + 1) * T, :], in_=hm[:]),
                      reads=[Rhm], dma=True, lane="o_" + Rhm.name)

        for i in range(NP):
            tile_S(i, False, None, True, allgrp=(i == NP - 1))
        for i in range(NF):
            tile_S(NP + i, True, i, i == 0)
        barrier()

    if "C" in phases:
        ws_reset()
        w1 = ws("w1", [T, 8, 4096], BF16)
        w2 = ws("w2", [T, 32, 1024], BF16)
        R_w1 = load_weight(w1, w_ff1, "w1", 8, 0, 4096)
        R_w2 = load_weight(w2, w_ff2, "w2", 32, 0, 1024)
        hrot2 = Rot("hC", [T, D], F32, 2)
        hnb = ws("hnb", [T, D], BF16); R_hnb = Res("hnb")
        h2T = ws("h2T", [T, 8, T], BF16); R_h2T = Res("h2T")
        rr = Rot("relu", [T, 512], BF16, 2)
        uT = ws("uT", [T, 32, T], BF16); R_uT = Res("uT")
        h2 = ws("h2", [T, D], F32); R_h2 = Res("h2")
        orot = Rot("ot", [T, D], F32, 2)

        def tile_C(i):
            Rhh, hh = hrot2.next()
            P.add("sp", lambda e: e.dma_start(out=hh[:], in_=hmid_d[i * T:(i + 1) * T, :]), writes=[Rhh], dma=True, lane=Rhh.name)
            R_st, st = rms_rstd(hh[:], Rhh, D)
            P.add("act", lambda e: e.activation(out=hnb[:], in_=hh[:], func=AF.Copy, scale=st[:, 2:3]),
                  reads=[Rhh, R_st], writes=[R_hnb])
            to_featmajor(hnb, R_hnb, nw[:, 24:32], h2T[:], R_h2T)
            for G in range(8):
                Rb, pb = bank()

                def f(e, pb=pb, G=G):
                    for j in range(4):
                        c0 = (G * 4 + j) * T
                        for kc in range(8):
                            ins = e.matmul(pb[:, j * T:(j + 1) * T], lhsT=w1[:, kc, c0:c0 + T], rhs=h2T[:, kc, :],
                                           start=(kc == 0), stop=(kc == 7), skip_group_check=True)
                    return ins
                P.add("pe", f, reads=[R_h2T, R_w1], writes=[Rb])
                Rr_, r_ = rr.next()
                P.add("act", lambda e, pb=pb, r_=r_: e.activation(out=r_[:], in_=pb, func=AF.Relu), reads=[Rb], writes=[Rr_])
                P.add("pool", lambda e, r_=r_, G=G: e.tensor_tensor(out=uT[:, G * 4:G * 4 + 4, :].rearrange("p a t -> p (a t)"),
                                                                     in0=r_[:], in1=r_[:], op=ALU.mult),
                      reads=[Rr_], writes=[R_uT])
            for nb in range(2):
                Rb, pb = bank()

                def f(e, pb=pb, nb=nb):
                    for fc in range(32):
                        ins = e.matmul(pb, lhsT=uT[:, fc, :], rhs=w2[:, fc, nb * 512:(nb + 1) * 512], start=(fc == 0), stop=(fc == 31))
                    return ins
                P.add("pe", f, reads=[R_uT, R_w2], writes=[Rb])
                P.add("dve", lambda e, pb=pb, nb=nb: e.tensor_tensor(out=h2[:, nb * 512:(nb + 1) * 512], in0=pb,
                                                                    in1=hh[:, nb * 512:(nb + 1) * 512], op=ALU.add),
                      reads=[Rb, Rhh], writes=[R_h2])
            R_st2, st2 = rms_rstd(h2[:], R_h2, D)
            Ro, ot = orot.next()
            P.add("dve", lambda e, ot=ot: e.scalar_tensor_tensor(out=ot[:], in0=h2[:], scalar=st2[:, 2:3], in1=C("finw"),
                                                                op0=ALU.mult, op1=ALU.mult),
                  reads=[R_h2, R_st2, R_cst], writes=[Ro])
            P.add("sp", lambda e, ot=ot: e.dma_start(out=out_d[i * T:(i + 1) * T, :], in_=ot[:]),
                  reads=[Ro], dma=True, lane="o_" + Ro.name, is_out=True)

        for i in range(NF):
            tile_C(i)

    import os
    P.schedule(reorder=os.environ.get("MK_NOREORDER") is None)
    print("[mk] estimated schedule us:", getattr(P, "est", None), "n_ops", len(P.ops))
    P.emit()
    return nc


def _geometry(seq):
    nchunk = seq // CH + 1
    TT = (nchunk + 1) // 2
    NF = (TT + 1) // 2
    return TT, NF


def prepare_inputs(x, meta_tokens, norm1_w, w_in, ret_norm_w, conv_w, conv_b, dt_bias, a_log,
                   d_skip, ssd_norm_w, w_out, norm2_w, w_ff1, w_ff2, final_norm_w):
    x = np.asarray(x, np.float32)
    B, seq, _ = x.shape
    TT, NF = _geometry(seq)
    NP = NF
    L = seq + CH
    Lp = 2 * NF * T
    hc = host_constants()
    idx = np.arange(Lp)
    pos = (idx - PAD).astype(np.float32)
    half = DK // 2
    freqs = (10000.0 ** (-np.arange(0, half, dtype=np.float32) / half)).astype(np.float32)
    ang = pos[:, None] * freqs[None, :]
    cos, sin = np.cos(ang).astype(np.float32), np.sin(ang).astype(np.float32)
    valid = ((idx >= PAD) & (idx < L)).astype(np.float32)
    cst = np.zeros((T, CST_N), np.float32)

    def put(name, arr):
        a, b = CST_LAY[name]
        cst[:, a:b] = arr
    nwt = np.concatenate([np.asarray(v, np.float32).reshape(8, T).T for v in
                          (norm1_w[0], ret_norm_w[0], ssd_norm_w[0], norm2_w[0])], axis=1)
    put("nw", nwt)
    cwv = np.asarray(conv_w, np.float32)[0]
    put("cw", cwv.reshape(4, 16, T).transpose(2, 1, 0).reshape(T, 64))
    put("cb", np.asarray(conv_b, np.float32)[0].reshape(16, T).T)
    put("dtb", np.broadcast_to(np.asarray(dt_bias, np.float32)[0][None, :], (T, 16)))
    put("alog", np.broadcast_to(np.asarray(a_log, np.float32)[0][None, :], (T, 16)))
    put("dsk", np.broadcast_to(np.asarray(d_skip, np.float32)[0][None, :], (T, 16)))
    put("U", hc["U"])
    put("ones", np.ones((T, T), np.float32))
    put("kd", hc["kd"])
    put("finw", np.broadcast_to(np.asarray(final_norm_w, np.float32)[None, :], (T, D)))
    in_maps = []
    meta = np.asarray(meta_tokens, np.float32)
    for b in range(B):
        hfull = np.zeros((Lp, D), np.float32)
        hfull[PAD:CH] = meta
        hfull[CH:L] = x[b]
        for h in range(2):
            xin = np.zeros(((NP + NF) * T, D), np.float32)
            aux = np.zeros(((NP + NF) * T, 260), np.float32)
            rows = slice(h * NF * T, (h + 1) * NF * T)

            def mkaux(r, vmask):
                a = np.zeros((NF * T, 260), np.float32)
                a[:, 0:64] = cos[r]
                a[:, 64:128] = sin[r]
                a[:, 128] = vmask
                a[:, 132:260] = vmask.reshape(NF, 1, T).repeat(T, axis=1).reshape(NF * T, T)
                return a
            if h == 1:
                xin[0:NP * T] = hfull[0:NF * T]
                aux[0:NP * T] = mkaux(slice(0, NF * T), valid[0:NF * T])
            else:
                aux[0:NP * T] = mkaux(slice(0, NF * T), np.zeros(NF * T, np.float32))
            xin[NP * T:] = hfull[rows]
            aux[NP * T:] = mkaux(rows, valid[rows])
            in_maps.append({
                "xin": xin, "aux": aux,
                "w_in": np.ascontiguousarray(np.asarray(w_in, np.float32)[0]),
                "w_out": np.ascontiguousarray(np.asarray(w_out, np.float32)[0]),
                "w_ff1": np.ascontiguousarray(np.asarray(w_ff1, np.float32)[0]),
                "w_ff2": np.ascontiguousarray(np.asarray(w_ff2, np.float32)[0]),
                "cst": cst, "cstb": hc["cstb"],
            })
    return in_maps, (B, seq, NF, NP, L, hc["sdec"])


_PROG_CACHE = {}


def kernel(**inputs):
    in_maps, (B, seq, NF, NP, L, sdec) = prepare_inputs(**inputs)
    key = (NP, NF)
    if key not in _PROG_CACHE:
        import os
        ph = tuple(os.environ.get("MK_PHASES", "R,S,C").split(","))
        _PROG_CACHE[key] = build_program(NP, NF, sdec, ph)
    nc = _PROG_CACHE[key]
    res = run_bass_kernel_spmd(nc, in_maps, core_ids=list(range(len(in_maps))))
    out = np.zeros((B, seq, D), np.float32)
    for b in range(B):
        full = np.concatenate([res.results[2 * b + h]["out"] for h in range(2)], axis=0)
        out[b] = full[CH:L]
    return out
```

```python
import math
import numpy as np
import ml_dtypes
import concourse.bass as bass
import concourse.mybir as mybir
from concourse.bass_utils import run_bass_kernel_spmd

F32 = mybir.dt.float32
BF16 = mybir.dt.bfloat16
AF = mybir.ActivationFunctionType
ALU = mybir.AluOpType
AX = mybir.AxisListType

D = 1024
CH = 64
N_META = 16
PAD = CH - N_META
EPS = 1e-6
T = 128
NH = 8
DK = 128
IN_COLS = 7184
D_FF = 4096


class Res:
    __slots__ = ("name", "last_w", "readers", "excl")

    def __init__(self, name, excl=False):
        self.name = name
        self.excl = excl
        self.last_w = None
        self.readers = []


class Op:
    __slots__ = ("eng", "fn", "deps", "dma", "lane", "signal", "sem", "val", "idx", "cost", "marker", "t0", "desc")


class _Dummy:
    def then_inc(self, *a, **k):
        return self


class _Probe:
    def __init__(self):
        self.calls = []

    def __getattr__(self, name):
        def f(*a, **k):
            self.calls.append((name, a, k))
            return _Dummy()
        return f


def _nfree(ap):
    n = 1
    for d_ in ap.shape[1:]:
        n *= d_
    return n


def _probe_cost(eng, fn, dma):
    p = _Probe()
    fn(p)
    c = 0.0
    for name, a, k in p.calls:
        if name == "matmul":
            rhs = k.get("rhs", a[2] if len(a) > 2 else None)
            N = _nfree(rhs)
            c += (max(N, 64) / 2400.0 + 0.028) * (4.0 if rhs.dtype == F32 else 1.0)
        elif name == "transpose":
            c += 0.085
        elif name == "dma_start":
            out = k.get("out", a[0] if a else None)
            nbytes = _nfree(out) * out.shape[0] * (4 if out.dtype == F32 else 2)
            c += 2.0 + nbytes / 120e3
        else:
            out = k.get("out", a[0] if a else None)
            n = _nfree(out) if out is not None else 64
            if eng == "act":
                c += 0.22 + n / 1150.0
            elif eng == "dve":
                c += 0.14 + n / 940.0
            elif eng == "pool":
                c += 0.25 + n / 480.0
            else:
                c += 0.1
    return c


class Prog:
    ENGS = ("pe", "act", "dve", "pool", "sp")

    def __init__(self, nc):
        self.nc = nc
        self.ops = []
        self.lanes = {}
        self.group_lanes = set()
        self.out_dmas = []

    def add(self, eng, fn, reads=(), writes=(), dma=False, lane=None, group=False, is_out=False, n=128, cost=None):
        op = Op()
        op.eng, op.fn, op.dma, op.lane = eng, fn, dma, lane
        op.marker = False
        if cost is None:
            cost = _probe_cost(eng, fn, dma)
        op.cost = cost
        op.t0 = 0.0
        op.desc = ""
        op.signal = False
        op.sem = None
        op.val = 0
        op.idx = len(self.ops)
        deps = set()
        excl = [r for r in reads if r.excl]
        reads = [r for r in reads if not r.excl]
        writes = list(writes) + excl
        for r in reads:
            if r.last_w is not None:
                deps.add(r.last_w)
        for w in writes:
            if w.last_w is not None:
                deps.add(w.last_w)
            for rd in w.readers:
                deps.add(rd)
        deps.discard(op)
        op.deps = deps
        for r in reads:
            r.readers.append(op)
        for w in writes:
            w.last_w = op
            w.readers = []
        if dma:
            assert lane is not None
            self.lanes.setdefault(lane, []).append(op)
            if group:
                self.group_lanes.add(lane)
            if is_out:
                self.out_dmas.append(op)
        self.ops.append(op)
        return op

    def barrier(self):
        op = Op()
        op.eng, op.fn, op.dma, op.lane = None, None, False, None
        op.marker = True
        op.deps = set()
        op.signal = False
        op.sem = None
        op.val = 0
        op.cost = 0
        op.idx = len(self.ops)
        self.ops.append(op)

    def _list_schedule(self, seg):
        import heapq
        inseg = set(id(o) for o in seg)
        ndeps = {}
        users = {}
        for o in seg:
            c = 0
            for d in o.deps:
                if id(d) in inseg:
                    c += 1
                    users.setdefault(id(d), []).append(o)
            ndeps[id(o)] = c
        finish = {}
        ready_t = {}
        future = {e: [] for e in self.ENGS}
        avail = {e: [] for e in self.ENGS}
        free = {e: 0.0 for e in self.ENGS}
        for o in seg:
            if ndeps[id(o)] == 0:
                heapq.heappush(future[o.eng], (0.0, o.idx, o))
        order = []
        n_left = len(seg)
        while n_left:
            best = None
            for e in self.ENGS:
                if avail[e]:
                    t = free[e]
                elif future[e]:
                    t = max(free[e], future[e][0][0])
                else:
                    continue
                if best is None or t < best[0]:
                    best = (t, e)
            t, e = best
            while future[e] and future[e][0][0] <= t:
                rt, idx, o = heapq.heappop(future[e])
                heapq.heappush(avail[e], (idx, rt, o))
            idx, rt, o = heapq.heappop(avail[e])
            start = max(t, rt)
            if o.dma:
                free[e] = start + 0.06
                fin = start + o.cost
            else:
                free[e] = start + o.cost
                fin = free[e]
            finish[id(o)] = fin
            o.t0 = start
            order.append(o)
            n_left -= 1
            for u in users.get(id(o), ()):
                ndeps[id(u)] -= 1
                r = max(ready_t.get(id(u), 0.0), fin)
                ready_t[id(u)] = r
                if ndeps[id(u)] == 0:
                    heapq.heappush(future[u.eng], (r, u.idx, u))
        self.est = getattr(self, "est", 0.0) + max(list(finish.values()) + [0.0])
        return order

    def schedule(self, reorder=True):
        segs = [[]]
        for o in self.ops:
            if o.marker:
                segs.append([])
            else:
                segs[-1].append(o)
        result = []
        seen_lanes = {}
        lasts = None
        for seg in segs:
            order = self._list_schedule(seg) if reorder else seg
            if lasts is not None:
                for e in self.ENGS:
                    b = Op()
                    b.eng, b.fn, b.dma, b.lane = e, (lambda eng: eng.nop()), False, None
                    b.marker = False
                    b.deps = set(lasts)
                    b.signal = False
                    b.sem = None
                    b.val = 0
                    b.cost = 0
                    b.idx = -1
                    result.append(b)
            result += order
            lasts = []
            for e in self.ENGS:
                for o in reversed(order):
                    if o.eng == e and not o.dma:
                        lasts.append(o)
                        break
            for o in order:
                if o.dma:
                    seen_lanes[o.lane] = o
            lasts += list(seen_lanes.values())
        self.ops = result

    def emit(self):
        nc = self.nc
        for op in self.ops:
            for d in op.deps:
                if not (d.eng == "pe" and op.eng == "pe" and not d.dma and not op.dma):
                    d.signal = True
        for op in self.out_dmas:
            op.signal = True
        import contextlib
        with contextlib.ExitStack() as st:
            esem = {e: st.enter_context(nc.semaphore("sem_" + e)) for e in self.ENGS}
            lsem = {l: st.enter_context(nc.semaphore("lane_" + l)) for l in self.lanes}
            cnt = {e: 0 for e in self.ENGS}
            lcnt = {l: 0 for l in self.lanes}
            for op in self.ops:
                if op.dma:
                    lcnt[op.lane] += 1
                    op.sem = lsem[op.lane]
                    op.val = 16 * lcnt[op.lane]
                elif op.signal:
                    cnt[op.eng] += 1
                    op.sem = esem[op.eng]
                    op.val = cnt[op.eng]
            for l in self.group_lanes:
                for op in self.lanes[l]:
                    op.val = 16 * lcnt[l]
            by_eng = {e: [o for o in self.ops if o.eng == e] for e in self.ENGS}
            block = st.enter_context(nc.Block())

            def run(engine, e):
                waited = {}
                for op in by_eng[e]:
                    needs = {}
                    for d in op.deps:
                        if d.eng == "pe" and e == "pe" and not d.dma and not op.dma:
                            continue
                        k = d.sem
                        if d.val > needs.get(k, (0, None))[0]:
                            needs[k] = (d.val, d.sem)
                    for k, (v, s) in needs.items():
                        if waited.get(k, 0) < v:
                            engine.wait_ge(s, v)
                            waited[k] = v
                    ins = op.fn(engine)
                    if op.dma:
                        ins.then_inc(op.sem, 16)
                    elif op.signal:
                        ins.then_inc(op.sem, 1)
                if e == "sp":
                    last = {}
                    for op in self.out_dmas:
                        last[op.sem] = max(last.get(op.sem, (0, None))[0], op.val), op.sem
                    for k, (v, s) in last.items():
                        engine.wait_ge(s, v)

            @block.tensor
            def _(t):
                run(t, "pe")

            @block.scalar
            def _(a):
                run(a, "act")

            @block.vector
            def _(v):
                run(v, "dve")

            @block.gpsimd
            def _(g):
                run(g, "pool")

            @block.sync
            def _(s):
                run(s, "sp")


CST_COLS = {}


def _cst_layout():
    off = 0
    lay = {}
    for name, n in (("nw", 32), ("cw", 64), ("cb", 16), ("dtb", 16), ("alog", 16), ("dsk", 16),
                    ("U", 128), ("ones", 128), ("kd", 8), ("finw", 1024)):
        lay[name] = (off, off + n)
        off += n
    return lay, off


CST_LAY, CST_N = _cst_layout()
CSTB_N = 128 + 512 + 1024 + 1024 + 128


def host_constants():
    lg = np.log1p(-np.exp2(-5.0 - np.arange(NH, dtype=np.float64)))
    n = np.arange(T)[None, :]
    m = np.arange(T)[:, None]
    same = (n // CH) == (m // CH)
    cross = (n >= CH) & (m < CH)
    DT = np.zeros((T, NH, T), np.float64)
    for h in range(NH):
        d = np.where(same, np.exp(lg[h] * np.abs(n - m)), 0.0)
        d = np.where(cross, np.exp(lg[h] * (n - m)), d)
        DT[:, h, :] = d * DK ** -0.5
    Gq = np.zeros((T, NH, T), np.float64)
    for h in range(NH):
        Gq[:, h, :] = (np.exp(lg[h] * (np.arange(T) + 1.0)) * DK ** -0.5)[None, :]
    kd = np.exp(lg[None, :] * (T - 1.0 - np.arange(T))[:, None])
    sdec = np.exp(lg * T)
    U = (np.arange(T)[:, None] <= np.arange(T)[None, :]).astype(np.float64)
    maskneg = np.where(np.arange(T)[None, :] >= np.arange(T)[:, None], 0.0, -30000.0)
    ident = np.eye(T)
    cstb = np.concatenate([ident, np.tile(maskneg, (1, 4)), DT.reshape(T, -1), Gq.reshape(T, -1), np.ones((T, T))], axis=1)
    return dict(U=U.astype(np.float32), kd=kd.astype(np.float32), sdec=[float(v) for v in sdec],
                cstb=cstb.astype(ml_dtypes.bfloat16))


def build_program(NP, NF, sdec, phases=("R", "S", "C")):
    nc = bass.Bass("TRN2", target_bir_lowering=False)
    NTI = NP + NF
    xin = nc.dram_tensor("xin", [NTI * T, D], F32, kind="ExternalInput").ap()
    aux = nc.dram_tensor("aux", [NTI * T, 260], F32, kind="ExternalInput").ap()
    w_in = nc.dram_tensor("w_in", [D, IN_COLS], F32, kind="ExternalInput").ap()
    w_out = nc.dram_tensor("w_out", [2 * D, D], F32, kind="ExternalInput").ap()
    w_ff1 = nc.dram_tensor("w_ff1", [D, D_FF], F32, kind="ExternalInput").ap()
    w_ff2 = nc.dram_tensor("w_ff2", [D_FF, D], F32, kind="ExternalInput").ap()
    cst_d = nc.dram_tensor("cst", [T, CST_N], F32, kind="ExternalInput").ap()
    cstb_d = nc.dram_tensor("cstb", [T, CSTB_N], BF16, kind="ExternalInput").ap()
    out_d = nc.dram_tensor("out", [NF * T, D], F32, kind="ExternalOutput").ap()
    yrT_d = nc.dram_tensor("yrT_scr", [NF * T, 8 * T], BF16, kind="Internal").ap()
    hmid_d = nc.dram_tensor("hmid_scr", [NF * T, D], F32, kind="Internal").ap()

    P = Prog(nc)
    _n = [0]

    def sb(name, shape, dt):
        _n[0] += 1
        return nc.alloc_sbuf_tensor(f"{name}_{_n[0]}", list(shape), dt).ap()

    ARENA_BYTES = 168 * 1024
    arena = nc.alloc_sbuf_tensor("arena", [T, ARENA_BYTES // 2], BF16).ap()
    ws_off = [0]

    def ws_reset():
        ws_off[0] = 0

    def ws(name, shape, dt):
        esz = 4 if dt == F32 else 2
        n = 1
        for d_ in shape[1:]:
            n *= d_
        nb = (n * esz + 31) // 32 * 32
        o = ws_off[0]
        assert o + nb <= ARENA_BYTES, (name, o, nb)
        ws_off[0] = o + nb
        a = arena[:, o // 2:o // 2 + (n * esz) // 2]
        if dt == F32:
            a = a.bitcast(F32)
        if len(shape) == 3:
            a = a.rearrange("p (a b) -> p a b", a=shape[1])
        return a

    class Rot:
        def __init__(self, name, shape, dt, n, alloc=None):
            alloc = alloc or ws
            self.slots = [(Res(f"{name}{i}"), alloc(f"{name}{i}", shape, dt)) for i in range(n)]
            self.i = 0

        def next(self):
            s = self.slots[self.i % len(self.slots)]
            self.i += 1
            return s

    ps_all = nc.alloc_psum_tensor("ps_all", [T, 8, 512], F32).ap()
    ps_res = [Res(f"bank{i}", excl=True) for i in range(8)]
    ps_i = [0]

    def bank():
        b = ps_i[0] % 8
        ps_i[0] += 1
        return ps_res[b], ps_all[:, b, :]

    cst = sb("cst", [T, CST_N], F32)
    cstb = sb("cstb", [T, CSTB_N], BF16)
    R_cst, R_cstb = Res("cst"), Res("cstb")
    P.add("sp", lambda e: e.dma_start(out=cst[:], in_=cst_d[:, :]), writes=[R_cst], dma=True, lane="cst")
    P.add("sp", lambda e: e.dma_start(out=cstb[:], in_=cstb_d[:, :]), writes=[R_cstb], dma=True, lane="cstb")

    def C(name):
        a, b = CST_LAY[name]
        return cst[:, a:b]

    ident = cstb[:, 0:128]
    maskneg4 = cstb[:, 128:640]
    DTt = cstb[:, 640:1664].rearrange("p (h n) -> p h n", h=NH)
    Gq = cstb[:, 1664:2688].rearrange("p (h n) -> p h n", h=NH)
    onesb = cstb[:, 2688:2816]
    nw = C("nw")
    aneg = sb("aneg", [T, 16], F32)
    R_aneg = Res("aneg")
    P.add("act", lambda e: e.activation(out=aneg[:], in_=C("alog"), func=AF.Exp), reads=[R_cst], writes=[R_aneg])
    P.add("dve", lambda e: e.tensor_scalar(out=aneg[:], in0=aneg[:], scalar1=-1.0, scalar2=None, op0=ALU.mult),
          reads=[R_aneg], writes=[R_aneg])

    junk = sb("junk", [T, D], F32)
    R_junk = Res("junk")
    small = sb("small", [T, 64], F32)
    R_small = Res("small")

    def barrier():
        P.barrier()

    def load_weight(dst3, src, lane, row_chunks, col0, ncols, ranges=None):
        Rw = Res(lane)
        if ranges is None:
            ranges = [(0, ncols)]
        for kc in range(row_chunks):
            for (r0, rn) in ranges:
                c = r0
                while c < r0 + rn:
                    n = min(2048, r0 + rn - c)
                    P.add("pool", (lambda e, kc=kc, c=c, n=n: e.dma_start(
                        out=dst3[:, kc, c:c + n], in_=src[kc * T:(kc + 1) * T, col0 + c:col0 + c + n])),
                        writes=[], dma=True, lane=lane, group=True)
                    c += n
        last = P.lanes[lane][-1]
        Rw.last_w = last
        Rw.readers = []
        return Rw

    xrot = Rot("xt", [T, D], F32, 2, sb)
    arot = Rot("ax", [T, 260], F32, 2, sb)
    xnrot = Rot("xn", [T, D], BF16, 2, sb)
    hrot = Rot("hnT", [T, 8, T], BF16, 2, sb)
    strot = Rot("st", [T, 16], F32, 4, sb)
    dumprot = Rot("dump", [T, D], BF16, 2, sb)

    def rms_rstd(src_ap, R_src, nfeat):
        R_st, st = strot.next()
        R_dump, dump = dumprot.next()
        P.add("act", lambda e: e.activation(out=dump[:, 0:nfeat], in_=src_ap, func=AF.Square, accum_out=st[:, 0:1]),
              reads=[R_src], writes=[R_st, R_dump])
        P.add("act", lambda e: e.activation(out=st[:, 1:2], in_=st[:, 0:1], func=AF.Ln, scale=1.0 / nfeat, bias=EPS),
              reads=[R_st], writes=[R_st])
        P.add("act", lambda e: e.activation(out=st[:, 2:3], in_=st[:, 1:2], func=AF.Exp, scale=-0.5),
              reads=[R_st], writes=[R_st])
        return R_st, st

    def to_featmajor(src_bf, R_src, wcols, dst, R_dst, nblk=8):
        Rb, pb = bank()
        pv = pb.bitcast(BF16)

        def f(e):
            for kc in range(nblk):
                ins = e.transpose(out=pv[:, kc * T:(kc + 1) * T], in_=src_bf[:, kc * T:(kc + 1) * T], identity=ident)
            return ins
        P.add("pe", f, reads=[R_src, R_cstb], writes=[Rb])
        P.add("dve", lambda e: e.tensor_tensor(
            out=dst, in0=pv[:, 0:nblk * T].rearrange("p (k t) -> p k t", k=nblk),
            in1=wcols.unsqueeze(2).to_broadcast([T, nblk, T]), op=ALU.mult),
            reads=[Rb, R_cst], writes=[R_dst])

    def load_and_norm(ti, wcol0):
        Rx, xt = xrot.next()
        Ra, ax = arot.next()
        P.add("sp", lambda e: e.dma_start(out=xt[:], in_=xin[ti * T:(ti + 1) * T, :]), writes=[Rx], dma=True, lane=Rx.name)
        P.add("sp", lambda e: e.dma_start(out=ax[:], in_=aux[ti * T:(ti + 1) * T, :]), writes=[Ra], dma=True, lane=Ra.name)
        R_st, st = rms_rstd(xt[:], Rx, D)
        R_xn, xn = xnrot.next()
        P.add("act", lambda e: e.activation(out=xn[:], in_=xt[:], func=AF.Copy, scale=st[:, 2:3]),
              reads=[Rx, R_st], writes=[R_xn])
        Rh, hnT = hrot.next()
        to_featmajor(xn, R_xn, nw[:, wcol0:wcol0 + 8], hnT[:], Rh)
        return Rx, xt, Ra, ax, Rh, hnT

    def proj_tokmajor(Rh, hnT, w3, Rw, col0, ncols=512):
        Rb, pb = bank()

        def f(e):
            for kc in range(8):
                ins = e.matmul(pb[:, 0:ncols], lhsT=hnT[:, kc, :], rhs=w3[:, kc, col0:col0 + ncols],
                               start=(kc == 0), stop=(kc == 7))
            return ins
        P.add("pe", f, reads=[Rh, Rw], writes=[Rb])
        return Rb, pb

    if "R" in phases:
        ws_reset()
        wR = ws("wR", [T, 8, 4096], BF16)
        R_wRkv = load_weight(wR, w_in, "wRkv", 8, 0, 4096, ranges=[(1024, 2048)])
        R_wRqg = load_weight(wR, w_in, "wRqg", 8, 0, 4096, ranges=[(0, 1024), (3072, 1024)])
        qrrot = Rot("qr", [T, D], BF16, 2)
        krrot = Rot("kr", [T, D], BF16, 2)
        tA = ws("tA", [T, 4, 64], F32); tB = ws("tB", [T, 4, 64], F32)
        R_tA, R_tB = Res("tA"), Res("tB")
        vrot = Rot("v", [T, D], BF16, 2)
        sgrot = Rot("sg", [T, D], BF16, 2)
        qTrot = Rot("qT", [T, NH, T], BF16, 2)
        qdTrot = Rot("qdT", [T, NH, T], BF16, 2)
        kTrot = Rot("kT", [T, NH, T], BF16, 2)
        kdrot = Rot("kdec", [T, D], BF16, 2)
        PTrot = Rot("PT", [T, NH, T], BF16, 2)
        S = ws("S", [T, NH, T], F32); R_S = Res("S")
        Sbf = ws("Sbf", [T, NH, T], BF16); R_Sbf = Res("Sbf")
        ss8rot = Rot("ss8", [T, 16], F32, 2)
        ytrot = Rot("ytmp", [T, D], F32, 2)
        yrrot_ = Rot("yr", [T, D], BF16, 2)
        yTrot = Rot("yT", [T, NH, T], BF16, 2)
        P.add("dve", lambda e: e.memset(S[:], 0.0), writes=[R_S])
        P.add("dve", lambda e: e.memset(Sbf[:], 0.0), writes=[R_Sbf])

        tC = ws("tC", [T, 4, 64], F32); tD = ws("tD", [T, 4, 64], F32)
        R_tC, R_tD = Res("tC"), Res("tD")
        qrawrot = Rot("qraw", [T, 512], F32, 2)

        def rope(Rb, pb, dst, R_dst, blk, Ra, ax, eng="dve"):
            if eng == "dve":
                tA_, tB_, R_tA_, R_tB_ = tA, tB, R_tA, R_tB
            else:
                tA_, tB_, R_tA_, R_tB_ = tC, tD, R_tC, R_tD
            pv = pb.rearrange("p (h two d) -> p h two d", h=4, two=2)
            a, b = pv[:, :, 0, :], pv[:, :, 1, :]
            cos = ax[:, 0:64].unsqueeze(1).to_broadcast([T, 4, 64])
            sin = ax[:, 64:128].unsqueeze(1).to_broadcast([T, 4, 64])
            dv = dst[:, blk * 512:(blk + 1) * 512].rearrange("p (h two d) -> p h two d", h=4, two=2)
            P.add(eng, lambda e: e.tensor_tensor(out=tA_[:], in0=a, in1=cos, op=ALU.mult), reads=[Rb, Ra], writes=[R_tA_])
            P.add(eng, lambda e: e.tensor_tensor(out=tB_[:], in0=b, in1=sin, op=ALU.mult), reads=[Rb, Ra], writes=[R_tB_])
            P.add(eng, lambda e: e.tensor_tensor(out=dv[:, :, 0, :], in0=tA_[:], in1=tB_[:], op=ALU.subtract),
                  reads=[R_tA_, R_tB_], writes=[R_dst])
            P.add(eng, lambda e: e.tensor_tensor(out=tA_[:], in0=a, in1=sin, op=ALU.mult), reads=[Rb, Ra], writes=[R_tA_])
            P.add(eng, lambda e: e.tensor_tensor(out=tB_[:], in0=b, in1=cos, op=ALU.mult), reads=[Rb, Ra], writes=[R_tB_])
            P.add(eng, lambda e: e.tensor_tensor(out=dv[:, :, 1, :], in0=tA_[:], in1=tB_[:], op=ALU.add),
                  reads=[R_tA_, R_tB_], writes=[R_dst])

        def tile_R(ti, full, out_i):
            Rx, xt, Ra, ax, Rh, hnT = load_and_norm(ti, 0)
            R_qr, qr = qrrot.next()
            R_kr, kr = krrot.next()
            R_qT, qT = qTrot.next()
            R_qdT, qdT = qdTrot.next()
            R_kT, kT = kTrot.next()
            R_kdec, kdec = kdrot.next()
            R_PT, PT = PTrot.next()
            R_ss8, ss8 = ss8rot.next()
            R_ytmp, ytmp = ytrot.next()
            R_yr, yr = yrrot_.next()
            Rv, v = vrot.next()
            Rsg, sg = sgrot.next()
            if full:
                for blk in range(2):
                    Rb, pb = proj_tokmajor(Rh, hnT, wR, R_wRqg, blk * 512)
                    Rq_, qraw = qrawrot.next()
                    P.add("act", lambda e, pb=pb, qraw=qraw: e.activation(out=qraw[:], in_=pb, func=AF.Copy),
                          reads=[Rb], writes=[Rq_])
                    rope(Rq_, qraw, qr, R_qr, blk, Ra, ax, eng="pool")
            for blk in range(2):
                Rb, pb = proj_tokmajor(Rh, hnT, wR, R_wRkv, 1024 + blk * 512)
                rope(Rb, pb, kr, R_kr, blk, Ra, ax)
            for blk in range(2):
                Rb, pb = proj_tokmajor(Rh, hnT, wR, R_wRkv, 2048 + blk * 512)
                P.add("act", lambda e, pb=pb, blk=blk: e.activation(out=v[:, blk * 512:(blk + 1) * 512], in_=pb, func=AF.Copy),
                      reads=[Rb], writes=[Rv])
            if full:
                for blk in range(2):
                    Rb, pb = proj_tokmajor(Rh, hnT, wR, R_wRqg, 3072 + blk * 512)
                    P.add("act", lambda e, pb=pb, blk=blk: e.activation(out=sg[:, blk * 512:(blk + 1) * 512], in_=pb, func=AF.Silu),
                          reads=[Rb], writes=[Rsg])
            P.add("dve", lambda e: e.tensor_tensor(
                out=kdec[:].rearrange("p (h d) -> p h d", h=NH), in0=kr[:].rearrange("p (h d) -> p h d", h=NH),
                in1=C("kd").unsqueeze(2).to_broadcast([T, NH, DK]), op=ALU.mult),
                reads=[R_kr, R_cst], writes=[R_kdec])
            if full:
                Rb, pb = bank()
                pv = pb.bitcast(BF16)

                def f(e, pv=pv):
                    for h in range(NH):
                        ins = e.transpose(out=pv[:, h * T:(h + 1) * T], in_=qr[:, h * DK:(h + 1) * DK], identity=ident)
                    return ins
                P.add("pe", f, reads=[R_qr, R_cstb], writes=[Rb])
                P.add("act", lambda e, pv=pv: e.activation(out=qT[:].rearrange("p h n -> p (h n)"), in_=pv, func=AF.Copy),
                      reads=[Rb], writes=[R_qT])
                P.add("dve", lambda e, pv=pv: e.tensor_tensor(out=qdT[:], in0=pv.rearrange("p (h n) -> p h n", h=NH),
                                                             in1=Gq, op=ALU.mult),
                      reads=[Rb, R_cstb], writes=[R_qdT])
                Rb, pb = bank()
                pv = pb.bitcast(BF16)

                def f(e, pv=pv):
                    for h in range(NH):
                        ins = e.transpose(out=pv[:, h * T:(h + 1) * T], in_=kr[:, h * DK:(h + 1) * DK], identity=ident)
                    return ins
                P.add("pe", f, reads=[R_kr, R_cstb], writes=[Rb])
                P.add("act", lambda e, pv=pv: e.activation(out=kT[:].rearrange("p h n -> p (h n)"), in_=pv, func=AF.Copy),
                      reads=[Rb], writes=[R_kT])
                for half in range(2):
                    Rb, pb = bank()

                    def f(e, pb=pb, half=half):
                        for j in range(4):
                            h = half * 4 + j
                            ins = e.matmul(pb[:, j * T:(j + 1) * T], lhsT=kT[:, h, :], rhs=qT[:, h, :], start=True, stop=True)
                        return ins
                    P.add("pe", f, reads=[R_kT, R_qT], writes=[Rb])
                    P.add("dve", lambda e, pb=pb, half=half: e.tensor_tensor(
                        out=PT[:, half * 4:half * 4 + 4, :], in0=pb.rearrange("p (h n) -> p h n", h=4),
                        in1=DTt[:, half * 4:half * 4 + 4, :], op=ALU.mult),
                        reads=[Rb, R_cstb], writes=[R_PT])
                obanks = []
                for half in range(2):
                    Rb, pb = bank()

                    def f(e, pb=pb, half=half):
                        for j in range(4):
                            h = half * 4 + j
                            e.matmul(pb[:, j * T:(j + 1) * T], lhsT=PT[:, h, :], rhs=v[:, h * DK:(h + 1) * DK],
                                     start=True, stop=False, skip_group_check=True)
                            ins = e.matmul(pb[:, j * T:(j + 1) * T], lhsT=qdT[:, h, :], rhs=Sbf[:, h, :],
                                           start=False, stop=True, skip_group_check=True)
                        return ins
                    P.add("pe", f, reads=[R_PT, Rv, R_qdT, R_Sbf], writes=[Rb])
                    obanks.append((Rb, pb))
            for half in range(2):
                Rb, pb = bank()

                def f(e, pb=pb, half=half):
                    for j in range(4):
                        h = half * 4 + j
                        ins = e.matmul(pb[:, j * T:(j + 1) * T], lhsT=kdec[:, h * DK:(h + 1) * DK], rhs=v[:, h * DK:(h + 1) * DK],
                                       start=True, stop=True)
                    return ins
                P.add("pe", f, reads=[R_kdec, Rv], writes=[Rb])
                for j in range(4):
                    h = half * 4 + j
                    P.add("dve", lambda e, pb=pb, j=j, h=h: e.scalar_tensor_tensor(
                        out=S[:, h, :], in0=S[:, h, :], scalar=sdec[h], in1=pb[:, j * T:(j + 1) * T],
                        op0=ALU.mult, op1=ALU.add), reads=[Rb, R_S], writes=[R_S])
            P.add("act", lambda e: e.activation(out=Sbf[:].rearrange("p h n -> p (h n)"),
                                                in_=S[:].rearrange("p h n -> p (h n)"), func=AF.Copy),
                  reads=[R_S], writes=[R_Sbf])
            if full:
                for half, (Rb, pb) in enumerate(obanks):
                    P.add("act", lambda e, pb=pb, half=half: e.activation(out=junk[:, half * 512:(half + 1) * 512], in_=pb, func=AF.Square),
                          reads=[Rb], writes=[R_junk])
                P.add("dve", lambda e: e.tensor_reduce(out=ss8[:, 0:8], in_=junk[:].rearrange("p (h d) -> p h d", h=NH),
                                                       axis=AX.X, op=ALU.add), reads=[R_junk], writes=[R_ss8])
                P.add("act", lambda e: e.activation(out=ss8[:, 8:16], in_=ss8[:, 0:8], func=AF.Ln, scale=1.0 / DK, bias=EPS),
                      reads=[R_ss8], writes=[R_ss8])
                P.add("act", lambda e: e.activation(out=ss8[:, 0:8], in_=ss8[:, 8:16], func=AF.Exp, scale=-0.5),
                      reads=[R_ss8], writes=[R_ss8])
                for half, (Rb, pb) in enumerate(obanks):
                    P.add("dve", lambda e, pb=pb, half=half: e.tensor_tensor(
                        out=ytmp[:, half * 512:(half + 1) * 512].rearrange("p (h d) -> p h d", h=4),
                        in0=pb.rearrange("p (h d) -> p h d", h=4),
                        in1=ss8[:, half * 4:half * 4 + 4].unsqueeze(2).to_broadcast([T, 4, DK]), op=ALU.mult),
                        reads=[Rb, R_ss8], writes=[R_ytmp])
                P.add("dve", lambda e: e.tensor_tensor(out=yr[:], in0=ytmp[:], in1=sg[:], op=ALU.mult),
                      reads=[R_ytmp, Rsg], writes=[R_yr])
                RyT, yT = yTrot.next()
                to_featmajor(yr, R_yr, nw[:, 8:16], yT[:], RyT)
                P.add("sp", lambda e: e.dma_start(out=yrT_d[out_i * T:(out_i + 1) * T, :], in_=yT[:].rearrange("p h n -> p (h n)")),
                      reads=[RyT], dma=True, lane="o_" + RyT.name)

        for i in range(NP):
            tile_R(i, False, None)
        for i in range(NF):
            tile_R(NP + i, True, i)
        barrier()

    if "S" in phases:
        ws_reset()
        wS = ws("wS", [T, 8, 3088], BF16)
        wO = ws("wO", [T, 16, 1024], BF16)
        R_wSa = load_weight(wS, w_in, "wSa", 8, 4096, 3088, ranges=[(1024, 1536), (3072, 16)])
        R_wSb = load_weight(wS, w_in, "wSb", 8, 4096, 3088, ranges=[(0, 1024), (2560, 512)])
        R_wO = load_weight(wO, w_out, "wO", 16, 0, 1024)
        szrot = Rot("sz", [T, D], BF16, 2)
        xcrot = Rot("xcb", [T, 4, 131], BF16, 2)
        halo = ws("halo", [T, 16, 3], BF16); R_halo = Res("halo")
        diag = ws("diag", [T, 80, T], BF16); R_diag = Res("diag")
        xcsrot = Rot("xcs", [T, 16, T], BF16, 2)
        dtsrot = Rot("dts", [T, 128], F32, 2)
        xs_tm = ws("xs_tm", [T, D], BF16); R_xs = Res("xs_tm")
        xdtrot = Rot("xdt", [T, D], BF16, 2)
        xdtdrot = Rot("xdtd", [T, D], BF16, 2)
        Btmrot = Rot("Btm", [T, 512], BF16, 2)
        Rrot = Rot("Rr", [T, 4, T], F32, 2)
        LT = ws("LT", [T, 16, T], BF16); R_LT = Res("LT")
        MT = ws("MT", [T, 16, T], BF16); R_MT = Res("MT")
        H = ws("H", [T, D], F32); R_H = Res("H")
        Hbf = ws("Hbf", [T, D], BF16); R_Hbf = Res("Hbf")
        y1 = ws("y1", [T, D], F32); R_y1 = Res("y1")
        y2 = junk; R_y2 = R_junk
        ynb = ws("ynb", [T, D], BF16); R_ynb = Res("ynb")
        ysTrot = Rot("ysT", [T, 8, T], BF16, 2)
        yrrot = Rot("yrT", [T, 8, T], BF16, 2)
        hmrot = Rot("hm", [T, D], F32, 1)
        P.add("dve", lambda e: e.memset(H[:], 0.0), writes=[R_H])
        P.add("dve", lambda e: e.memset(Hbf[:], 0.0), writes=[R_Hbf])
        P.add("dve", lambda e: e.memset(halo[:], 0.0), writes=[R_halo])
        cw = C("cw").rearrange("p (b j) -> p b j", j=4)
        cbias = C("cb")
        for blk in range(16):
            for k in range(5):
                sc = cw[:, blk, k:k + 1] if k < 4 else cbias[:, blk:blk + 1]
                P.add("dve", lambda e, blk=blk, k=k, sc=sc: e.tensor_scalar(
                    out=diag[:, blk * 5 + k, :], in0=ident, scalar1=sc, scalar2=None, op0=ALU.mult),
                    reads=[R_cst, R_cstb], writes=[R_diag])

        def tile_S(ti, full, out_i, mask, allgrp=False):
            Rx, xt, Ra, ax, Rh, hnT = load_and_norm(ti, 0)
            R_xcs, xcs = xcsrot.next()
            R_dts, dts = dtsrot.next()
            R_xdt, xdt = xdtrot.next()
            R_xdtd, xdtd = xdtdrot.next()
            R_Btm, Btm = Btmrot.next()
            R_ysT, ysT = ysTrot.next()
            Rsz, sz = szrot.next()
            if full:
                for blk in range(2):
                    Rb, pb = proj_tokmajor(Rh, hnT, wS, R_wSb, blk * 512)
                    P.add("act", lambda e, pb=pb, blk=blk: e.activation(out=sz[:, blk * 512:(blk + 1) * 512], in_=pb, func=AF.Silu),
                          reads=[Rb], writes=[Rsz])
            Rb, pb = proj_tokmajor(Rh, hnT, wS, R_wSa, 3072, 16)
            P.add("dve", lambda e, pb=pb: e.tensor_tensor(out=dts[:, 0:16], in0=pb[:, 0:16], in1=C("dtb"), op=ALU.add),
                  reads=[Rb, R_cst], writes=[R_dts])
            P.add("act", lambda e: e.activation(out=dts[:, 0:16], in_=dts[:, 0:16], func=AF.Exp), reads=[R_dts], writes=[R_dts])
            P.add("act", lambda e: e.activation(out=dts[:, 0:16], in_=dts[:, 0:16], func=AF.Ln, bias=1.0), reads=[R_dts], writes=[R_dts])
            P.add("dve", lambda e: e.tensor_scalar(out=dts[:, 16:32], in0=dts[:, 0:16], scalar1=ax[:, 128:129], scalar2=None, op0=ALU.mult),
                  reads=[R_dts, Ra], writes=[R_dts])
            P.add("dve", lambda e: e.tensor_tensor(out=dts[:, 32:48], in0=dts[:, 16:32], in1=aneg[:], op=ALU.mult),
                  reads=[R_dts, R_aneg], writes=[R_dts])
            Rb, pb = bank()

            def f(e, pb=pb):
                e.matmul(pb[:, 0:16], lhsT=C("U"), rhs=dts[:, 32:48], start=True, stop=True)
                return e.matmul(pb[:, 16:32], lhsT=C("ones"), rhs=dts[:, 32:48], start=True, stop=True)
            P.add("pe", f, reads=[R_dts, R_cst], writes=[Rb])
            P.add("act", lambda e, pb=pb: e.activation(out=dts[:, 48:64], in_=pb[:, 0:16], func=AF.Copy), reads=[Rb], writes=[R_dts])
            P.add("act", lambda e, pb=pb: e.activation(out=dts[:, 64:80], in_=pb[:, 0:16], func=AF.Copy, scale=-1.0), reads=[Rb], writes=[R_dts])
            P.add("dve", lambda e, pb=pb: e.tensor_tensor(out=dts[:, 80:96], in0=pb[:, 16:32], in1=dts[:, 48:64], op=ALU.subtract),
                  reads=[Rb, R_dts], writes=[R_dts])
            P.add("act", lambda e: e.activation(out=dts[:, 80:96], in_=dts[:, 80:96], func=AF.Exp), reads=[R_dts], writes=[R_dts])
            P.add("act", lambda e, pb=pb: e.activation(out=dts[:, 96:112], in_=pb[:, 16:32], func=AF.Exp), reads=[Rb], writes=[R_dts])
            if full:
                P.add("act", lambda e: e.activation(out=dts[:, 112:128], in_=dts[:, 48:64], func=AF.Exp), reads=[R_dts], writes=[R_dts])
            ngrp = 4 if (full or allgrp) else 3
            for g4 in range(ngrp):
                Rb, pb = bank()

                def f(e, pb=pb, g4=g4):
                    for j in range(4):
                        c0 = 1024 + (g4 * 4 + j) * T
                        for kc in range(8):
                            ins = e.matmul(pb[:, j * T:(j + 1) * T], lhsT=wS[:, kc, c0:c0 + T], rhs=hnT[:, kc, :],
                                           start=(kc == 0), stop=(kc == 7), skip_group_check=True)
                    return ins
                P.add("pe", f, reads=[Rh, (R_wSb if g4 == 3 else R_wSa)], writes=[Rb])
                Rxc, xc = xcrot.next()
                P.add("act", lambda e, pb=pb, xc=xc: e.activation(out=xc[:, :, 3:131], in_=pb.rearrange("p (j t) -> p j t", j=4), func=AF.Copy),
                      reads=[Rb], writes=[Rxc])
                P.add("pool", lambda e, xc=xc, g4=g4: e.tensor_copy(out=xc[:, :, 0:3], in_=halo[:, g4 * 4:g4 * 4 + 4, :]),
                      reads=[R_halo], writes=[Rxc])
                P.add("pool", lambda e, xc=xc, g4=g4: e.tensor_copy(out=halo[:, g4 * 4:g4 * 4 + 4, :], in_=xc[:, :, 128:131]),
                      reads=[Rxc], writes=[R_halo])
                Rc, pc = bank()

                def f(e, pc=pc, xc=xc, g4=g4):
                    for j in range(4):
                        blk = g4 * 4 + j
                        for k in range(4):
                            e.matmul(pc[:, j * T:(j + 1) * T], lhsT=diag[:, blk * 5 + k, :], rhs=xc[:, j, k:k + T],
                                     start=(k == 0), stop=False, skip_group_check=True)
                        ins = e.matmul(pc[:, j * T:(j + 1) * T], lhsT=diag[:, blk * 5 + 4, :], rhs=onesb,
                                       start=False, stop=True, skip_group_check=True)
                    return ins
                P.add("pe", f, reads=[Rxc, R_diag, R_cstb], writes=[Rc])
                if mask:
                    P.add("act", lambda e, pc=pc: e.activation(out=y2[:, 0:512], in_=pc, func=AF.Silu), reads=[Rc], writes=[R_y2])
                    P.add("dve", lambda e, g4=g4, xcs=xcs: e.tensor_tensor(
                        out=xcs[:, g4 * 4:g4 * 4 + 4, :], in0=y2[:, 0:512].rearrange("p (j t) -> p j t", j=4),
                        in1=ax[:, 132:260].unsqueeze(1).to_broadcast([T, 4, T]),
                        op=ALU.mult), reads=[R_y2, Ra], writes=[R_xcs])
                else:
                    P.add("act", lambda e, pc=pc, g4=g4, xcs=xcs: e.activation(
                        out=xcs[:, g4 * 4:g4 * 4 + 4, :].rearrange("p j t -> p (j t)"), in_=pc, func=AF.Silu),
                        reads=[Rc], writes=[R_xcs])
            Rb, pb = bank()
            pv = pb.bitcast(BF16)

            def f(e, pv=pv):
                for j in range(8):
                    ins = e.transpose(out=pv[:, j * T:(j + 1) * T], in_=xcs[:, j, :], identity=ident)
                return ins
            P.add("pe", f, reads=[R_xcs, R_cstb], writes=[Rb])
            if full:
                P.add("act", lambda e, pv=pv: e.activation(out=xs_tm[:], in_=pv, func=AF.Copy), reads=[Rb], writes=[R_xs])
            P.add("dve", lambda e, pv=pv: e.tensor_tensor(
                out=xdt[:].rearrange("p (h d) -> p h d", h=16), in0=pv.rearrange("p (h d) -> p h d", h=16),
                in1=dts[:, 16:32].unsqueeze(2).to_broadcast([T, 16, 64]), op=ALU.mult),
                reads=[Rb, R_dts], writes=[R_xdt])
            P.add("dve", lambda e: e.tensor_tensor(
                out=xdtd[:].rearrange("p (h d) -> p h d", h=16), in0=xdt[:].rearrange("p (h d) -> p h d", h=16),
                in1=dts[:, 80:96].unsqueeze(2).to_broadcast([T, 16, 64]), op=ALU.mult),
                reads=[R_xdt, R_dts], writes=[R_xdtd])
            Rb, pb = bank()
            pv = pb.bitcast(BF16)

            def f(e, pv=pv):
                for j in range(4):
                    ins = e.transpose(out=pv[:, j * T:(j + 1) * T], in_=xcs[:, 8 + j, :], identity=ident)
                return ins
            P.add("pe", f, reads=[R_xcs, R_cstb], writes=[Rb])
            P.add("act", lambda e, pv=pv: e.activation(out=Btm[:], in_=pv[:, 0:512], func=AF.Copy), reads=[Rb], writes=[R_Btm])
            if full:
                for g4 in range(4):
                    Rr, Rt = Rrot.next()
                    P.add("pool", lambda e, Rt=Rt, g4=g4: e.tensor_tensor(
                        out=Rt[:], in0=C("U").unsqueeze(1).to_broadcast([T, 4, T]),
                        in1=dts[:, 32 + g4 * 4:32 + g4 * 4 + 4].unsqueeze(2).to_broadcast([T, 4, T]), op=ALU.mult),
                        reads=[R_cst, R_dts], writes=[Rr])
                    Rb, pb = bank()

                    def f(e, pb=pb, Rt=Rt):
                        e.matmul(pb, lhsT=C("ones"), rhs=Rt[:].rearrange("p h l -> p (h l)"), start=True, stop=False)
                        return e.matmul(pb, lhsT=ident, rhs=maskneg4, start=False, stop=True)
                    P.add("pe", f, reads=[Rr, R_cst, R_cstb], writes=[Rb])
                    for j in range(4):
                        hh = g4 * 4 + j
                        P.add("act", lambda e, pb=pb, j=j, hh=hh: e.activation(
                            out=LT[:, hh, :], in_=pb[:, j * T:(j + 1) * T], func=AF.Exp, bias=dts[:, 64 + hh:65 + hh]),
                            reads=[Rb, R_dts], writes=[R_LT])
                Rb, pb = bank()

                def f(e, pb=pb):
                    for g in range(4):
                        ins = e.matmul(pb[:, g * T:(g + 1) * T], lhsT=xcs[:, 8 + g, :], rhs=xcs[:, 12 + g, :], start=True, stop=True)
                    return ins
                P.add("pe", f, reads=[R_xcs], writes=[Rb])
                for g in range(4):
                    P.add("dve", lambda e, pb=pb, g=g: e.tensor_tensor(
                        out=MT[:, g * 4:g * 4 + 4, :], in0=LT[:, g * 4:g * 4 + 4, :],
                        in1=pb[:, g * T:(g + 1) * T].unsqueeze(1).to_broadcast([T, 4, T]), op=ALU.mult),
                        reads=[Rb, R_LT], writes=[R_MT])
                for half in range(2):
                    RbY, pbY = bank()

                    def f(e, pb=pbY, half=half):
                        for j in range(8):
                            h = half * 8 + j
                            ins = e.matmul(pb[:, j * 64:(j + 1) * 64], lhsT=MT[:, h, :], rhs=xdt[:, h * 64:(h + 1) * 64],
                                           start=True, stop=True)
                        return ins
                    P.add("pe", f, reads=[R_MT, R_xdt], writes=[RbY])
                    RbO, pbO = bank()

                    def f(e, pb=pbO, half=half):
                        for j in range(2):
                            g = half * 2 + j
                            ins = e.matmul(pb[:, j * 256:(j + 1) * 256], lhsT=xcs[:, 12 + g, :], rhs=Hbf[:, g * 256:(g + 1) * 256],
                                           start=True, stop=True)
                        return ins
                    P.add("pe", f, reads=[R_xcs, R_Hbf], writes=[RbO])
                    hs = slice(half * 512, (half + 1) * 512)
                    P.add("dve", lambda e, pbO=pbO, half=half, hs=hs: e.tensor_tensor(
                        out=y1[:, hs].rearrange("p (h d) -> p h d", h=8), in0=pbO.rearrange("p (h d) -> p h d", h=8),
                        in1=dts[:, 112 + half * 8:120 + half * 8].unsqueeze(2).to_broadcast([T, 8, 64]), op=ALU.mult),
                        reads=[RbO, R_dts], writes=[R_y1])
                    P.add("dve", lambda e, pbY=pbY, hs=hs: e.tensor_tensor(out=y1[:, hs], in0=pbY, in1=y1[:, hs], op=ALU.add),
                          reads=[RbY, R_y1], writes=[R_y1])
                    P.add("pool", lambda e, half=half, hs=hs: e.tensor_tensor(
                        out=y2[:, hs].rearrange("p (h d) -> p h d", h=8), in0=xs_tm[:, hs].rearrange("p (h d) -> p h d", h=8),
                        in1=C("dsk")[:, half * 8:half * 8 + 8].unsqueeze(2).to_broadcast([T, 8, 64]), op=ALU.mult),
                        reads=[R_xs, R_cst], writes=[R_y2])
                    P.add("pool", lambda e, hs=hs: e.tensor_tensor(out=y1[:, hs], in0=y1[:, hs], in1=y2[:, hs], op=ALU.add),
                          reads=[R_y1, R_y2], writes=[R_y1])
                    P.add("pool", lambda e, hs=hs: e.tensor_tensor(out=y1[:, hs], in0=y1[:, hs], in1=sz[:, hs], op=ALU.mult),
                          reads=[R_y1, Rsz], writes=[R_y1])
            for half in range(2):
                Rb, pb = bank()

                def f(e, pb=pb, half=half):
                    for j in range(2):
                        g = half * 2 + j
                        ins = e.matmul(pb[:, j * 256:(j + 1) * 256], lhsT=Btm[:, g * T:(g + 1) * T], rhs=xdtd[:, g * 256:(g + 1) * 256],
                                       start=True, stop=True)
                    return ins
                P.add("pe", f, reads=[R_Btm, R_xdtd], writes=[Rb])
                hs = slice(half * 512, (half + 1) * 512)
                P.add("dve", lambda e, half=half, hs=hs: e.tensor_tensor(
                    out=H[:, hs].rearrange("p (h d) -> p h d", h=8), in0=H[:, hs].rearrange("p (h d) -> p h d", h=8),
                    in1=dts[:, 96 + half * 8:104 + half * 8].unsqueeze(2).to_broadcast([T, 8, 64]), op=ALU.mult),
                    reads=[R_H, R_dts], writes=[R_H])
                P.add("dve", lambda e, pb=pb, hs=hs: e.tensor_tensor(out=H[:, hs], in0=pb, in1=H[:, hs], op=ALU.add),
                      reads=[Rb, R_H], writes=[R_H])
            P.add("act", lambda e: e.activation(out=Hbf[:], in_=H[:], func=AF.Copy), reads=[R_H], writes=[R_Hbf])
            if full:
                P.add("act", lambda e: e.activation(out=junk[:], in_=y1[:], func=AF.Square), reads=[R_y1], writes=[R_junk])
                P.add("dve", lambda e: e.tensor_reduce(out=small[:, 0:4], in_=junk[:].rearrange("p (g d) -> p g d", g=4),
                                                       axis=AX.X, op=ALU.add), reads=[R_junk], writes=[R_small])
                P.add("act", lambda e: e.activation(out=small[:, 4:8], in_=small[:, 0:4], func=AF.Ln, scale=1.0 / 256, bias=EPS),
                      reads=[R_small], writes=[R_small])
                P.add("act", lambda e: e.activation(out=small[:, 8:12], in_=small[:, 4:8], func=AF.Exp, scale=-0.5),
                      reads=[R_small], writes=[R_small])
                P.add("dve", lambda e: e.tensor_tensor(
                    out=ynb[:].rearrange("p (g d) -> p g d", g=4), in0=y1[:].rearrange("p (g d) -> p g d", g=4),
                    in1=small[:, 8:12].unsqueeze(2).to_broadcast([T, 4, 256]), op=ALU.mult),
                    reads=[R_y1, R_small], writes=[R_ynb])
                to_featmajor(ynb, R_ynb, nw[:, 16:24], ysT[:], R_ysT)
                Ryr, yrT = yrrot.next()
                P.add("sp", lambda e: e.dma_start(out=yrT[:].rearrange("p h n -> p (h n)"), in_=yrT_d[out_i * T:(out_i + 1) * T, :]),
                      writes=[Ryr], dma=True, lane=Ryr.name)
                Rhm, hm = hmrot.next()
                for nb in range(2):
                    Rb, pb = bank()

                    def f(e, pb=pb, nb=nb, yrT=yrT):
                        for kc in range(16):
                            l = yrT[:, kc, :] if kc < 8 else ysT[:, kc - 8, :]
                            ins = e.matmul(pb, lhsT=l, rhs=wO[:, kc, nb * 512:(nb + 1) * 512], start=(kc == 0), stop=(kc == 15))
                        return ins
                    P.add("pe", f, reads=[Ryr, R_ysT, R_wO], writes=[Rb])
                    P.add("dve", lambda e, pb=pb, nb=nb, hm=hm: e.tensor_tensor(
                        out=hm[:, nb * 512:(nb + 1) * 512], in0=pb, in1=xt[:, nb * 512:(nb + 1) * 512], op=ALU.add),
                        reads=[Rb, Rx], writes=[Rhm])
                P.add("sp", lambda e, hm=hm: e.dma_start(out=hmid_d[out_i * T:(out_i + 1) * T, :], in_=hm[:]),
                      reads=[Rhm], dma=True, lane="o_" + Rhm.name)

        for i in range(NP):
            tile_S(i, False, None, True, allgrp=(i == NP - 1))
        for i in range(NF):
            tile_S(NP + i, True, i, i == 0)
        barrier()

    if "C" in phases:
        ws_reset()
        w1 = ws("w1", [T, 8, 4096], BF16)
        w2 = ws("w2", [T, 32, 1024], BF16)
        R_w1 = load_weight(w1, w_ff1, "w1", 8, 0, 4096)
        R_w2 = load_weight(w2, w_ff2, "w2", 32, 0, 1024)
        hrot2 = Rot("hC", [T, D], F32, 2)
        hnb = ws("hnb", [T, D], BF16); R_hnb = Res("hnb")
        h2T = ws("h2T", [T, 8, T], BF16); R_h2T = Res("h2T")
        rr = Rot("relu", [T, 512], BF16, 2)
        uT = ws("uT", [T, 32, T], BF16); R_uT = Res("uT")
        h2 = ws("h2", [T, D], F32); R_h2 = Res("h2")
        orot = Rot("ot", [T, D], F32, 2)

        def tile_C(i):
            Rhh, hh = hrot2.next()
            P.add("sp", lambda e: e.dma_start(out=hh[:], in_=hmid_d[i * T:(i + 1) * T, :]), writes=[Rhh], dma=True, lane=Rhh.name)
            R_st, st = rms_rstd(hh[:], Rhh, D)
            P.add("act", lambda e: e.activation(out=hnb[:], in_=hh[:], func=AF.Copy, scale=st[:, 2:3]),
                  reads=[Rhh, R_st], writes=[R_hnb])
            to_featmajor(hnb, R_hnb, nw[:, 24:32], h2T[:], R_h2T)
            for G in range(8):
                Rb, pb = bank()

                def f(e, pb=pb, G=G):
                    for j in range(4):
                        c0 = (G * 4 + j) * T
                        for kc in range(8):
                            ins = e.matmul(pb[:, j * T:(j + 1) * T], lhsT=w1[:, kc, c0:c0 + T], rhs=h2T[:, kc, :],
                                           start=(kc == 0), stop=(kc == 7), skip_group_check=True)
                    return ins
                P.add("pe", f, reads=[R_h2T, R_w1], writes=[Rb])
                Rr_, r_ = rr.next()
                P.add("act", lambda e, pb=pb, r_=r_: e.activation(out=r_[:], in_=pb, func=AF.Relu), reads=[Rb], writes=[Rr_])
                P.add("pool", lambda e, r_=r_, G=G: e.tensor_tensor(out=uT[:, G * 4:G * 4 + 4, :].rearrange("p a t -> p (a t)"),
                                                                     in0=r_[:], in1=r_[:], op=ALU.mult),
                      reads=[Rr_], writes=[R_uT])
            for nb in range(2):
                Rb, pb = bank()

                def f(e, pb=pb, nb=nb):
                    for fc in range(32):
                        ins = e.matmul(pb, lhsT=uT[:, fc, :], rhs=w2[:, fc, nb * 512:(nb + 1) * 512], start=(fc == 0), stop=(fc == 31))
                    return ins
                P.add("pe", f, reads=[R_uT, R_w2], writes=[Rb])
                P.add("dve", lambda e, pb=pb, nb=nb: e.tensor_tensor(out=h2[:, nb * 512:(nb + 1) * 512], in0=pb,
                                                                    in1=hh[:, nb * 512:(nb + 1) * 512], op=ALU.add),
                      reads=[Rb, Rhh], writes=[R_h2])
            R_st2, st2 = rms_rstd(h2[:], R_h2, D)
            Ro, ot = orot.next()
            P.add("dve", lambda e, ot=ot: e.scalar_tensor_tensor(out=ot[:], in0=h2[:], scalar=st2[:, 2:3], in1=C("finw"),
                                                                op0=ALU.mult, op1=ALU.mult),
                  reads=[R_h2, R_st2, R_cst], writes=[Ro])
            P.add("sp", lambda e, ot=ot: e.dma_start(out=out_d[i * T:(i + 1) * T, :], in_=ot[:]),
                  reads=[Ro], dma=True, lane="o_" + Ro.name, is_out=True)

        for i in range(NF):
            tile_C(i)

    import os
    P.schedule(reorder=os.environ.get("MK_NOREORDER") is None)
    print("[mk] estimated schedule us:", getattr(P, "est", None), "n_ops", len(P.ops))
    P.emit()
    return nc


def _geometry(seq):
    nchunk = seq // CH + 1
    TT = (nchunk + 1) // 2
    NF = (TT + 1) // 2
    return TT, NF


def prepare_inputs(x, meta_tokens, norm1_w, w_in, ret_norm_w, conv_w, conv_b, dt_bias, a_log,
                   d_skip, ssd_norm_w, w_out, norm2_w, w_ff1, w_ff2, final_norm_w):
    x = np.asarray(x, np.float32)
    B, seq, _ = x.shape
    TT, NF = _geometry(seq)
    NP = NF
    L = seq + CH
    Lp = 2 * NF * T
    hc = host_constants()
    idx = np.arange(Lp)
    pos = (idx - PAD).astype(np.float32)
    half = DK // 2
    freqs = (10000.0 ** (-np.arange(0, half, dtype=np.float32) / half)).astype(np.float32)
    ang = pos[:, None] * freqs[None, :]
    cos, sin = np.cos(ang).astype(np.float32), np.sin(ang).astype(np.float32)
    valid = ((idx >= PAD) & (idx < L)).astype(np.float32)
    cst = np.zeros((T, CST_N), np.float32)

    def put(name, arr):
        a, b = CST_LAY[name]
        cst[:, a:b] = arr
    nwt = np.concatenate([np.asarray(v, np.float32).reshape(8, T).T for v in
                          (norm1_w[0], ret_norm_w[0], ssd_norm_w[0], norm2_w[0])], axis=1)
    put("nw", nwt)
    cwv = np.asarray(conv_w, np.float32)[0]
    put("cw", cwv.reshape(4, 16, T).transpose(2, 1, 0).reshape(T, 64))
    put("cb", np.asarray(conv_b, np.float32)[0].reshape(16, T).T)
    put("dtb", np.broadcast_to(np.asarray(dt_bias, np.float32)[0][None, :], (T, 16)))
    put("alog", np.broadcast_to(np.asarray(a_log, np.float32)[0][None, :], (T, 16)))
    put("dsk", np.broadcast_to(np.asarray(d_skip, np.float32)[0][None, :], (T, 16)))
    put("U", hc["U"])
    put("ones", np.ones((T, T), np.float32))
    put("kd", hc["kd"])
    put("finw", np.broadcast_to(np.asarray(final_norm_w, np.float32)[None, :], (T, D)))
    in_maps = []
    meta = np.asarray(meta_tokens, np.float32)
    for b in range(B):
        hfull = np.zeros((Lp, D), np.float32)
        hfull[PAD:CH] = meta
        hfull[CH:L] = x[b]
        for h in range(2):
            xin = np.zeros(((NP + NF) * T, D), np.float32)
            aux = np.zeros(((NP + NF) * T, 260), np.float32)
            rows = slice(h * NF * T, (h + 1) * NF * T)

            def mkaux(r, vmask):
                a = np.zeros((NF * T, 260), np.float32)
                a[:, 0:64] = cos[r]
                a[:, 64:128] = sin[r]
                a[:, 128] = vmask
                a[:, 132:260] = vmask.reshape(NF, 1, T).repeat(T, axis=1).reshape(NF * T, T)
                return a
            if h == 1:
                xin[0:NP * T] = hfull[0:NF * T]
                aux[0:NP * T] = mkaux(slice(0, NF * T), valid[0:NF * T])
            else:
                aux[0:NP * T] = mkaux(slice(0, NF * T), np.zeros(NF * T, np.float32))
            xin[NP * T:] = hfull[rows]
            aux[NP * T:] = mkaux(rows, valid[rows])
            in_maps.append({
                "xin": xin, "aux": aux,
                "w_in": np.ascontiguousarray(np.asarray(w_in, np.float32)[0]),
                "w_out": np.ascontiguousarray(np.asarray(w_out, np.float32)[0]),
                "w_ff1": np.ascontiguousarray(np.asarray(w_ff1, np.float32)[0]),
                "w_ff2": np.ascontiguousarray(np.asarray(w_ff2, np.float32)[0]),
                "cst": cst, "cstb": hc["cstb"],
            })
    return in_maps, (B, seq, NF, NP, L, hc["sdec"])


_PROG_CACHE = {}


def kernel(**inputs):
    in_maps, (B, seq, NF, NP, L, sdec) = prepare_inputs(**inputs)
    key = (NP, NF)
    if key not in _PROG_CACHE:
        import os
        ph = tuple(os.environ.get("MK_PHASES", "R,S,C").split(","))
        _PROG_CACHE[key] = build_program(NP, NF, sdec, ph)
    nc = _PROG_CACHE[key]
    res = run_bass_kernel_spmd(nc, in_maps, core_ids=list(range(len(in_maps))))
    out = np.zeros((B, seq, D), np.float32)
    for b in range(B):
        full = np.concatenate([res.results[2 * b + h]["out"] for h in range(2)], axis=0)
        out[b] = full[CH:L]
    return out
```
